# Optimizing a Trainium2 kernel written in Bass

```python
import math
import numpy as np
import jax
import jax.numpy as jnp
from jax import lax

D_MODEL = 2048
BATCH = 4
SEQ = 4096
DEPTH = 2

GRID_W = 64
CTX_LEN = 256
NA_HEAD_DIM = 128
NA_WIDTH = D_MODEL // 2
NA_HEADS = NA_WIDTH // NA_HEAD_DIM
WIN_ROWS = 8
WIN_COLS = 16
MLA_V_DIM = 128
MLA_HEADS = (D_MODEL - NA_WIDTH) // MLA_V_DIM
MLA_NOPE_DIM = 128
MLA_ROPE_DIM = 64
MLA_QK_DIM = MLA_NOPE_DIM + MLA_ROPE_DIM
MLA_Q_LORA = D_MODEL // 4
MLA_KV_LORA = D_MODEL // 4
DIFF_HEAD_DIM = 128
DIFF_V_DIM = 2 * DIFF_HEAD_DIM
DIFF_HEADS = D_MODEL // DIFF_V_DIM
DIFF_QK_W = DIFF_HEADS * 2 * DIFF_HEAD_DIM
MLP_HIDDEN = 4 * D_MODEL
ROPE_BASE = 10000.0
NORM_EPS = 1e-6
Q_BLOCK = 128
N_EVEN = (DEPTH + 1) // 2
N_ODD = DEPTH // 2
AB_Q_SIZES = (NA_WIDTH, MLA_Q_LORA)
AB_KV_SIZES = (NA_WIDTH, NA_WIDTH, MLA_KV_LORA, MLA_ROPE_DIM)
AB_Q_COLS = sum(AB_Q_SIZES)
AB_IN_COLS = AB_Q_COLS + sum(AB_KV_SIZES)
DIFF_IN_COLS = 2 * DIFF_QK_W + DIFF_HEADS * DIFF_V_DIM

F32 = jnp.float32

kernel_name = "hybrid_natten_mla_diffattn_prefix_dit"


def rms_norm(x, g):
    xf = x.astype(F32)
    y = xf * lax.rsqrt(jnp.mean(xf * xf, axis=-1, keepdims=True) + NORM_EPS)
    return (y * g.astype(F32)).astype(x.dtype)


def modulate(h, shift, scale):
    return h * (1 + scale) + shift


def split_sizes(t, sizes):
    return jnp.split(t, np.cumsum(sizes)[:-1].tolist(), axis=-1)


def split_heads(t, n_heads):
    b, n, _ = t.shape
    return t.reshape(b, n, n_heads, -1).transpose(0, 2, 1, 3)


def merge_heads(t):
    b, h, n, d = t.shape
    return t.transpose(0, 2, 1, 3).reshape(b, n, h * d)


def axial_rope_angles(n, rot_dim):
    axis_dim = rot_dim // 2
    inv_freq = ROPE_BASE ** (-jnp.arange(0, axis_dim, 2, dtype=F32) / axis_dim)
    t = jnp.arange(n, dtype=jnp.int32)
    row = (t // GRID_W).astype(F32)
    col = (t % GRID_W).astype(F32)
    return row[:, None] * inv_freq, col[:, None] * inv_freq


def rotate(x, ang):
    x1, x2 = jnp.split(x.astype(F32), 2, axis=-1)
    cos, sin = jnp.cos(ang), jnp.sin(ang)
    return jnp.concatenate([x1 * cos - x2 * sin, x1 * sin + x2 * cos], axis=-1).astype(x.dtype)


def apply_axial_rope(x, ang):
    xr, xc = jnp.split(x, 2, axis=-1)
    return jnp.concatenate([rotate(xr, ang[0]), rotate(xc, ang[1])], axis=-1)


def softmax_attend(q, k, v, scale):
    s = jnp.einsum('bhqd,bhkd->bhqk', q, k, preferred_element_type=F32) * scale
    p = jax.nn.softmax(s, axis=-1)
    return jnp.einsum('bhqk,bhkd->bhqd', p.astype(v.dtype), v)


def sweep_query_blocks(block_fn, qs):
    b, h, n, _ = qs[0].shape
    nb = n // Q_BLOCK
    blocks = tuple(q.reshape(b, h, nb, Q_BLOCK, q.shape[-1]).transpose(2, 0, 1, 3, 4) for q in qs)
    out = lax.map(lambda blk: block_fn(*blk), blocks)
    return out.transpose(1, 2, 0, 3, 4).reshape(b, h, n, out.shape[-1])


def neighborhood_attention(q, k, v, k_ctx, v_ctx, rpb):
    b, h, n, dh = q.shape
    rows = n // GRID_W
    kr = min(WIN_ROWS, rows)
    scale = dh ** -0.5
    kg = k.reshape(b, h, rows, GRID_W, dh)
    vg = v.reshape(b, h, rows, GRID_W, dh)
    q_rows = jnp.moveaxis(q.reshape(b, h, rows, GRID_W, dh), 2, 0)
    col = jnp.arange(GRID_W)
    col_start = jnp.clip(col - WIN_COLS // 2, 0, GRID_W - WIN_COLS)
    col_in = (col[None, :] >= col_start[:, None]) & (col[None, :] < col_start[:, None] + WIN_COLS)
    dc_idx = jnp.clip(col[None, :] - col[:, None], 1 - WIN_COLS, WIN_COLS - 1) + WIN_COLS - 1

    def row_step(args):
        q_row, r = args
        r_start = jnp.clip(r - kr // 2, 0, rows - kr)
        k_blk = lax.dynamic_slice_in_dim(kg, r_start, kr, axis=2)
        v_blk = lax.dynamic_slice_in_dim(vg, r_start, kr, axis=2)
        dr_idx = r_start + jnp.arange(kr) - r + WIN_ROWS - 1
        bias = rpb[:, dr_idx[None, :, None], dc_idx[:, None, :]].astype(F32)
        s_loc = jnp.einsum('bhqd,bhrkd->bhqrk', q_row, k_blk, preferred_element_type=F32) * scale + bias
        s_loc = jnp.where(col_in[:, None, :], s_loc, -jnp.inf).reshape(b, h, GRID_W, kr * GRID_W)
        s_ctx = jnp.einsum('bhqd,bhcd->bhqc', q_row, k_ctx, preferred_element_type=F32) * scale
        p = jax.nn.softmax(jnp.concatenate([s_loc, s_ctx], axis=-1), axis=-1).astype(v.dtype)
        p_loc = p[..., :kr * GRID_W].reshape(b, h, GRID_W, kr, GRID_W)
        p_ctx = p[..., kr * GRID_W:]
        return (jnp.einsum('bhqrk,bhrkd->bhqd', p_loc, v_blk)
                + jnp.einsum('bhqc,bhcd->bhqd', p_ctx, v_ctx))

    out = lax.map(row_step, (q_rows, jnp.arange(rows)))
    return jnp.moveaxis(out, 0, 2).reshape(b, h, n, dh)


def mla_queries(c_q, g_cq, w_uq, ang):
    q = split_heads(rms_norm(c_q, g_cq) @ w_uq, MLA_HEADS)
    q_nope, q_rope = q[..., :MLA_NOPE_DIM], q[..., MLA_NOPE_DIM:]
    if ang is not None:
        q_rope = apply_axial_rope(q_rope, ang)
    return jnp.concatenate([q_nope, q_rope], axis=-1)


def mla_keys_values(c_kv, k_r, g_ckv, w_ukv, ang):
    kv = split_heads(rms_norm(c_kv, g_ckv) @ w_ukv, MLA_HEADS)
    k_nope, v = kv[..., :MLA_NOPE_DIM], kv[..., MLA_NOPE_DIM:]
    k_rope = k_r[:, None]
    if ang is not None:
        k_rope = apply_axial_rope(k_rope, ang)
    k_rope = jnp.broadcast_to(k_rope, k_nope.shape[:-1] + (MLA_ROPE_DIM,))
    return jnp.concatenate([k_nope, k_rope], axis=-1), v


def mixer_ab(h, hc, w_in, g_cq, g_ckv, w_uq, w_ukv, rpb, w_out, ang, ctx_out):
    q_a, c_q, k_a, v_a, c_kv, k_r = split_sizes(h @ w_in, AB_Q_SIZES + AB_KV_SIZES)
    if ctx_out:
        qc_a, cc_q, kc_a, vc_a, cc_kv, kc_r = split_sizes(hc @ w_in, AB_Q_SIZES + AB_KV_SIZES)
    else:
        kc_a, vc_a, cc_kv, kc_r = split_sizes(hc @ w_in[:, AB_Q_COLS:], AB_KV_SIZES)
    kc_a, vc_a = split_heads(kc_a, NA_HEADS), split_heads(vc_a, NA_HEADS)
    o_a = neighborhood_attention(split_heads(q_a, NA_HEADS), split_heads(k_a, NA_HEADS),
                                 split_heads(v_a, NA_HEADS), kc_a, vc_a, rpb)
    mla_scale = MLA_QK_DIM ** -0.5
    k_b, v_b = mla_keys_values(c_kv, k_r, g_ckv, w_ukv, ang)
    kc_b, vc_b = mla_keys_values(cc_kv, kc_r, g_ckv, w_ukv, None)
    k_all = jnp.concatenate([k_b, kc_b], axis=2)
    v_all = jnp.concatenate([v_b, vc_b], axis=2)
    q_b = mla_queries(c_q, g_cq, w_uq, ang)
    o_b = sweep_query_blocks(lambda qb: softmax_attend(qb, k_all, v_all, mla_scale), (q_b,))
    y = jnp.concatenate([merge_heads(o_a), merge_heads(o_b)], axis=-1) @ w_out
    if not ctx_out:
        return y, None
    oc_a = softmax_attend(split_heads(qc_a, NA_HEADS), kc_a, vc_a, NA_HEAD_DIM ** -0.5)
    oc_b = softmax_attend(mla_queries(cc_q, g_cq, w_uq, None), kc_b, vc_b, mla_scale)
    yc = jnp.concatenate([merge_heads(oc_a), merge_heads(oc_b)], axis=-1) @ w_out
    return y, yc


def diff_pairs(t, ang):
    b, n, _ = t.shape
    t = t.reshape(b, n, DIFF_HEADS, 2, DIFF_HEAD_DIM).transpose(0, 2, 3, 1, 4)
    if ang is not None:
        t = apply_axial_rope(t, ang)
    return t[:, :, 0], t[:, :, 1]


def diff_attend(q1, q2, k1, k2, v, lam):
    scale = DIFF_HEAD_DIM ** -0.5
    a1 = jax.nn.softmax(jnp.einsum('bhqd,bhkd->bhqk', q1, k1, preferred_element_type=F32) * scale, axis=-1)
    a2 = jax.nn.softmax(jnp.einsum('bhqd,bhkd->bhqk', q2, k2, preferred_element_type=F32) * scale, axis=-1)
    return jnp.einsum('bhqk,bhkd->bhqd', (a1 - lam * a2).astype(v.dtype), v)


def mixer_diff(h, hc, w_in, lq1, lk1, lq2, lk2, subln_g, w_out, lambda_init, ang, ctx_out):
    lam = (jnp.exp(jnp.sum(lq1.astype(F32) * lk1.astype(F32)))
           - jnp.exp(jnp.sum(lq2.astype(F32) * lk2.astype(F32))) + lambda_init)
    q, k, v = split_sizes(h @ w_in, (DIFF_QK_W, DIFF_QK_W, DIFF_HEADS * DIFF_V_DIM))
    if ctx_out:
        qc, kc, vc = split_sizes(hc @ w_in, (DIFF_QK_W, DIFF_QK_W, DIFF_HEADS * DIFF_V_DIM))
    else:
        kc, vc = split_sizes(hc @ w_in[:, DIFF_QK_W:], (DIFF_QK_W, DIFF_HEADS * DIFF_V_DIM))
    q1, q2 = diff_pairs(q, ang)
    k1, k2 = diff_pairs(k, ang)
    kc1, kc2 = diff_pairs(kc, None)
    vc = split_heads(vc, DIFF_HEADS)
    k1_all = jnp.concatenate([k1, kc1], axis=2)
    k2_all = jnp.concatenate([k2, kc2], axis=2)
    v_all = jnp.concatenate([split_heads(v, DIFF_HEADS), vc], axis=2)
    o = sweep_query_blocks(lambda a, b: diff_attend(a, b, k1_all, k2_all, v_all, lam), (q1, q2))
    y = merge_heads(rms_norm(o, subln_g) * (1 - lambda_init)) @ w_out
    if not ctx_out:
        return y, None
    qc1, qc2 = diff_pairs(qc, None)
    oc = diff_attend(qc1, qc2, kc1, kc2, vc, lam)
    yc = merge_heads(rms_norm(oc, subln_g) * (1 - lambda_init)) @ w_out
    return y, yc


def squared_relu_mlp(h, w1, w2):
    return jnp.square(jax.nn.relu(h @ w1)) @ w2


def setup_inputs(seed: int = 0) -> dict:
    key = jax.random.key(seed)
    ks = list(jax.random.split(key, 32))
    counter = iter(range(32))

    def nrm(shape, s):
        return jax.random.normal(ks[next(counter)], shape, F32) * s

    def w(shape):
        return nrm(shape, shape[-2] ** -0.5)

    d = D_MODEL
    return {
        "x": nrm((BATCH, SEQ, d), 1.0),
        "c": nrm((BATCH, d), 1.0),
        "ctx": nrm((BATCH, CTX_LEN, d), 1.0),
        "c_ctx": nrm((d,), 1.0),
        "ada_w": w((DEPTH, d, 6 * d)),
        "ada_b": nrm((DEPTH, 6 * d), 0.02),
        "norm1_g": 1.0 + nrm((DEPTH, d), 0.02),
        "norm2_g": 1.0 + nrm((DEPTH, d), 0.02),
        "mlp_w1": w((DEPTH, d, MLP_HIDDEN)),
        "mlp_w2": w((DEPTH, MLP_HIDDEN, d)),
        "ab_w_in": w((N_EVEN, d, AB_IN_COLS)),
        "ab_g_cq": 1.0 + nrm((N_EVEN, MLA_Q_LORA), 0.02),
        "ab_g_ckv": 1.0 + nrm((N_EVEN, MLA_KV_LORA), 0.02),
        "ab_w_uq": w((N_EVEN, MLA_Q_LORA, MLA_HEADS * MLA_QK_DIM)),
        "ab_w_ukv": w((N_EVEN, MLA_KV_LORA, MLA_HEADS * (MLA_NOPE_DIM + MLA_V_DIM))),
        "ab_rpb": nrm((N_EVEN, NA_HEADS, 2 * WIN_ROWS - 1, 2 * WIN_COLS - 1), 0.1),
        "ab_w_out": w((N_EVEN, NA_WIDTH + MLA_HEADS * MLA_V_DIM, d)),
        "diff_w_in": w((N_ODD, d, DIFF_IN_COLS)),
        "diff_lq1": nrm((N_ODD, DIFF_HEAD_DIM), 0.1),
        "diff_lk1": nrm((N_ODD, DIFF_HEAD_DIM), 0.1),
        "diff_lq2": nrm((N_ODD, DIFF_HEAD_DIM), 0.1),
        "diff_lk2": nrm((N_ODD, DIFF_HEAD_DIM), 0.1),
        "diff_subln_g": 1.0 + nrm((N_ODD, DIFF_V_DIM), 0.02),
        "diff_w_out": w((N_ODD, DIFF_HEADS * DIFF_V_DIM, d)),
        "final_g": 1.0 + nrm((d,), 0.02),
    }


def reference(x, c, ctx, c_ctx, ada_w, ada_b, norm1_g, norm2_g, mlp_w1, mlp_w2,
              ab_w_in, ab_g_cq, ab_g_ckv, ab_w_uq, ab_w_ukv, ab_rpb, ab_w_out,
              diff_w_in, diff_lq1, diff_lk1, diff_lq2, diff_lk2, diff_subln_g, diff_w_out,
              final_g):
    n = x.shape[1]
    ang_mla = axial_rope_angles(n, MLA_ROPE_DIM)
    ang_diff = axial_rope_angles(n, DIFF_HEAD_DIM)
    silu_c = jax.nn.silu(c)
    silu_cc = jax.nn.silu(c_ctx)
    for l in range(DEPTH):
        last = l == DEPTH - 1
        i = l // 2
        sh1, sc1, g1, sh2, sc2, g2 = jnp.split((silu_c @ ada_w[l] + ada_b[l])[:, None, :], 6, axis=-1)
        csh1, csc1, cg1, csh2, csc2, cg2 = jnp.split(silu_cc @ ada_w[l] + ada_b[l], 6, axis=-1)
        h = modulate(rms_norm(x, norm1_g[l]), sh1, sc1)
        hc = modulate(rms_norm(ctx, norm1_g[l]), csh1, csc1)
        if l % 2 == 0:
            y, yc = mixer_ab(h, hc, ab_w_in[i], ab_g_cq[i], ab_g_ckv[i], ab_w_uq[i], ab_w_ukv[i],
                             ab_rpb[i], ab_w_out[i], ang_mla, not last)
        else:
            lambda_init = 0.8 - 0.6 * math.exp(-0.3 * l)
            y, yc = mixer_diff(h, hc, diff_w_in[i], diff_lq1[i], diff_lk1[i], diff_lq2[i], diff_lk2[i],
                               diff_subln_g[i], diff_w_out[i], lambda_init, ang_diff, not last)
        x = x + g1 * y
        x = x + g2 * squared_relu_mlp(modulate(rms_norm(x, norm2_g[l]), sh2, sc2), mlp_w1[l], mlp_w2[l])
        if not last:
            ctx = ctx + cg1 * yc
            ctx = ctx + cg2 * squared_relu_mlp(modulate(rms_norm(ctx, norm2_g[l]), csh2, csc2),
                                               mlp_w1[l], mlp_w2[l])
    return rms_norm(x, final_g)
```

```python
import numpy as np
from contextlib import ExitStack
import concourse.bass as bass
import concourse.mybir as mybir
from concourse.bass_utils import run_bass_kernel_spmd

F32 = mybir.dt.float32
BF16 = mybir.dt.bfloat16
AF = mybir.ActivationFunctionType
ALU = mybir.AluOpType

COMPUTE = ("pe", "act", "dve", "pool")
ALLENG = ("pe", "act", "dve", "pool", "sp")

D = 2048
NLAT = 4096
NCTX = 256
NT = NLAT + NCTX
GROUPS = [(g * 512, 512) for g in range(8)] + [(NLAT, NCTX)]
EPS = 1e-6
NEG = -30000.0
N_CORES = 4


class MK:
    def __init__(self, nc, es):
        self.nc = nc
        self.es = es
        self.h = {"pe": nc.tensor, "act": nc.scalar, "dve": nc.vector, "pool": nc.gpsimd, "sp": nc.sync}
        self.ops = []
        self.last_w = {}
        self.readers = {}
        self.last_op = {}
        self.dcount = {}

    def _deps(self, i, eng, is_dma, reads, writes, grp=None):
        raw = set()
        oth = set()
        for b in reads:
            d = self.last_w.get(b)
            if d is not None:
                raw.add(d)
        for b in writes:
            d = self.last_w.get(b)
            if d is not None:
                oth.add(d)
            for r in self.readers.get(b, ()):
                oth.add(r)
        deps = []
        for d in sorted(raw | oth):
            od = self.ops[d]
            if od[4] is not None:
                deps.append(("d", od[4], self.dcount[od[4]]))
                continue
            if (not is_dma) and od[0] == eng and eng == "pe":
                continue
            od[3] = True
            deps.append(("c", d))
        grp = grp if grp is not None else eng
        for b in reads:
            lst = self.readers.setdefault(b, [])
            lst[:] = [r for r in lst if (self.ops[r][4] if self.ops[r][4] is not None else self.ops[r][0]) != grp]
            lst.append(i)
        for b in writes:
            self.last_w[b] = i
            self.readers[b] = []
        return deps

    def op(self, eng, fn, reads=(), writes=()):
        i = len(self.ops)
        deps = self._deps(i, eng, False, reads, writes)
        self.ops.append([eng, fn, deps, False, None])
        self.last_op[eng] = i
        return i

    def dma(self, eng, key, out, in_, reads=(), writes=()):
        i = len(self.ops)
        deps = self._deps(i, eng, True, reads, writes, grp=key)
        if self.dcount.get(key, 0) > 0:
            deps.append(("d", key, self.dcount[key]))
        self.dcount[key] = self.dcount.get(key, 0) + 16
        self.ops.append([eng, (lambda e, o=out, a=in_: e.dma_start(out=o, in_=a)), deps, False, key])
        return i

    def barrier(self):
        deps = []
        for e in COMPUTE:
            d = self.last_op.get(e)
            if d is not None:
                self.ops[d][3] = True
                deps.append(("c", d))
        for k, v in self.dcount.items():
            deps.append(("d", k, v))
        for e in ALLENG:
            self.ops.append([e, None, list(deps), False, None])
        self.last_w = {}
        self.readers = {}

    def emit(self):
        nc = self.nc
        if not hasattr(self, "esem"):
            self.esem = {e: self.es.enter_context(nc.semaphore("sem_" + e)) for e in COMPUTE}
            self.dsem = {}
            self.cnt = {e: 0 for e in COMPUTE}
            self.tok = {}
            self.waited = {}
            self.pos = 0
        esem, dsem, cnt, tok, waited = self.esem, self.dsem, self.cnt, self.tok, self.waited
        for k in self.dcount:
            if k not in dsem:
                dsem[k] = self.es.enter_context(nc.semaphore("dq_" + k))
        for i in range(self.pos, len(self.ops)):
            eng, fn, deps, signal, dkey = self.ops[i]
            h = self.h[eng]
            for dep in deps:
                if dep[0] == "c":
                    se, val = tok[dep[1]]
                    sem = esem[se]
                    kk = (eng, "c" + se)
                else:
                    sem = dsem[dep[1]]
                    val = dep[2]
                    kk = (eng, "d" + dep[1])
                if waited.get(kk, 0) >= val:
                    continue
                waited[kk] = val
                h.wait_ge(sem, val)
            if fn is None:
                continue
            ins = fn(h)
            if dkey is not None:
                ins.then_inc(dsem[dkey], 16)
            elif signal:
                cnt[eng] += 1
                ins.then_inc(esem[eng], 1)
                tok[i] = (eng, cnt[eng])
        self.pos = len(self.ops)

    def sync(self):
        self.barrier()
        self.emit()


def _rope_tables(rot_dim):
    axis_dim = rot_dim // 2
    half = axis_dim // 2
    inv_freq = (10000.0 ** (-np.arange(0, axis_dim, 2, dtype=np.float32) / np.float32(axis_dim))).astype(np.float32)
    t = np.arange(NLAT)
    row = (t // 64).astype(np.float32)
    col = (t % 64).astype(np.float32)
    ang_r = row[:, None] * inv_freq[None, :]
    ang_c = col[:, None] * inv_freq[None, :]
    cos = np.ones((rot_dim, NT), np.float32)
    sin = np.zeros((rot_dim, NT), np.float32)
    for base, ang in ((0, ang_r), (axis_dim, ang_c)):
        c = np.cos(ang).astype(np.float32).T
        s = np.sin(ang).astype(np.float32).T
        cos[base:base + half, :NLAT] = c
        cos[base + half:base + axis_dim, :NLAT] = c
        sin[base:base + half, :NLAT] = -s
        sin[base + half:base + axis_dim, :NLAT] = s
    perm = np.zeros((rot_dim, rot_dim), np.float32)
    for base in (0, axis_dim):
        for f in range(half):
            perm[base + f + half, base + f] = 1.0
            perm[base + f, base + f + half] = 1.0
    return cos, sin, perm


def _na_plan():
    def r_start(r):
        return int(np.clip(r - 4, 0, 56))
    def c_start(c):
        return np.clip(c - 8, 0, 48)
    qi = np.arange(128)
    cases = {}
    tiles = []
    plan = []
    for j in range(32):
        lst = []
        qr = 2 * j + qi // 64
        qc = qi % 64
        rs = np.array([r_start(r) for r in qr])
        cs = c_start(qc)
        for i in range(32):
            kr = 2 * i + qi // 64
            kc = qi % 64
            valid = ((kr[None, :] >= rs[:, None]) & (kr[None, :] < rs[:, None] + 8) &
                     (kc[None, :] >= cs[:, None]) & (kc[None, :] < cs[:, None] + 16))
            if not valid.any():
                continue
            key = ("i", i - j) if 2 <= j <= 29 else ("e", j, i)
            if key not in cases:
                dr = np.clip(kr[None, :] - qr[:, None] + 7, 0, 14)
                dc = np.clip(kc[None, :] - qc[:, None], -15, 15) + 15
                cases[key] = len(tiles)
                tiles.append((valid, dr, dc))
            lst.append((i, cases[key]))
        plan.append(lst)
    return plan, tiles


NA_PLAN, NA_TILES = _na_plan()
NBT = len(NA_TILES)


def build(stop_after=None, debug_out=()):
    nc = bass.Bass("TRN2", target_bir_lowering=False)
    dbg = {}

    def din(name, shape, dt=F32):
        return nc.dram_tensor(name, list(shape), dt, kind="ExternalInput").ap()

    def dscr(name, shape, dt):
        if name in debug_out:
            t = nc.dram_tensor(name, list(shape), dt, kind="ExternalOutput").ap()
            dbg[name] = t
            return t
        return nc.dram_tensor(name, list(shape), dt).ap()

    x_in = din("x", [NLAT, D])
    ctx_in = din("ctx", [NCTX, D])
    cvec = din("cvec", [2, D])
    ada_w = din("ada_w", [2, D, 6 * D])
    ada_b = din("ada_b", [2, 6 * D])
    gtab = din("gtab", [80, 128])
    g512 = din("g512", [8, 128])
    subg_in = din("subg", [2, 128])
    lvec = din("lvec", [128, 4])
    btab = din("btab", [8, NBT, 128, 128])
    cosm_in = din("cosm", [64, NT]); sinm_in = din("sinm", [64, NT]); perm64_in = din("perm64", [64, 64])
    cosd_in = din("cosd", [128, NT]); sind_in = din("sind", [128, NT]); perm128_in = din("perm128", [128, 128])
    ident_in = din("ident", [128, 128])
    w_in0 = din("ab_w_in", [D, 4160])
    w_uq = din("ab_w_uq", [512, 1536])
    w_ukv = din("ab_w_ukv", [512, 2048])
    w_out0 = din("ab_w_out", [D, D])
    mlp_w1 = din("mlp_w1", [2, D, 8192])
    mlp_w2 = din("mlp_w2", [2, 8192, D])
    w_in1 = din("diff_w_in", [D, 6144])
    w_out1 = din("diff_w_out", [D, D])
    out_d = nc.dram_tensor("out", [NLAT, D], F32, kind="ExternalOutput").ap()

    XT = dscr("XT", [16, 128, NT], F32)
    QA = dscr("QA", [8, 128, NT], BF16)
    KA = dscr("KA", [8, 128, NT], BF16)
    VA = dscr("VA", [NT, 1024], BF16)
    CQ = dscr("CQ", [4, 128, NT], F32)
    CKV = dscr("CKV", [4, 128, NT], F32)
    KR = dscr("KR", [64, NT], F32)
    OT = dscr("OT", [16, 128, NT], BF16)
    QD = dscr("QD", [16, 128, NLAT], BF16)
    KD = dscr("KD", [16, 128, NT], BF16)
    VD = dscr("VD", [NT, 2048], BF16)

    es = ExitStack()
    mk = MK(nc, es)
    lam_init = 0.8 - 0.6 * float(np.exp(-0.3))

    with es:
        uid = [0]

        def sb(name, shape, dt):
            uid[0] += 1
            return es.enter_context(nc.sbuf_tensor(f"sb{uid[0]}_{name}", list(shape), dt))

        castc = [0]

        class WS:
            def __init__(self, name, w_ap, K, blocks):
                self.kc = K // 128
                self.blocks = blocks
                self.t = []
                for bi, (c0, ncol) in enumerate(blocks):
                    t = nc.dram_tensor(f"{name}_{bi}", [128, self.kc, ncol], BF16).ap()
                    self.t.append(t)
                    src = w_ap[:, c0:c0 + ncol].rearrange("(k p) n -> p k n", p=128)
                    for k0 in range(0, self.kc, 16):
                        k1 = min(self.kc, k0 + 16)
                        sl = castc[0] % 4
                        castc[0] += 1
                        mk.dma("pool", f"cast{sl}", t[:, k0:k1, :], src[:, k0:k1, :], writes=[("castslot", sl)])
                self.name = name

        def blocks_of(n, nb):
            return [(c, min(nb, n - c)) for c in range(0, n, nb)]

        ws_in0 = WS("w_in0", w_in0, D, blocks_of(4096, 512) + [(4096, 64)])
        ws_uq = WS("w_uq", w_uq, 512, blocks_of(1536, 1536))
        ws_ukv = WS("w_ukv", w_ukv, 512, blocks_of(2048, 2048))
        ws_out0 = WS("w_out0", w_out0, D, blocks_of(D, 512))
        ws_w1 = [WS(f"w1_{l}", mlp_w1[l], D, blocks_of(8192, 512)) for l in range(2)]
        ws_w2 = [WS(f"w2_{l}", mlp_w2[l], 8192, blocks_of(D, 128)) for l in range(2)]
        ws_in1 = WS("w_in1", w_in1, D, blocks_of(6144, 512))
        ws_out1 = WS("w_out1", w_out1, D, blocks_of(D, 512))

        ident = sb("ident", [128, 128], F32)
        identb = sb("identb", [128, 128], BF16)
        onesb = sb("onesb", [128, 128], BF16)
        onesf = sb("onesf", [128, 128], F32)
        epst = sb("epst", [128, 1], F32)
        modT = sb("modT", [128, 2, 96, 2], F32)
        gT = sb("gT", [128, 80], F32)
        g5T = sb("g5T", [128, 8], F32)
        subgT = sb("subgT", [128, 2], F32)
        AS = sb("AS", [128, 2, 2, 2, 16], F32)
        lamt = sb("lamt", [128, 4], F32)
        PBt = [es.enter_context(nc.psum_tensor(f"psb{i}", [128, 512], F32)) for i in range(7)]
        PTt = es.enter_context(nc.psum_tensor("ptb", [128, 1024], BF16))

        def PB(i):
            return PBt[i], f"pb{i}"

        mk.dma("sp", "k_id", ident[:], ident_in, writes=["ident"])
        mk.op("dve", lambda e: e.tensor_copy(out=identb[:], in_=ident[:]), reads=["ident"], writes=["identb"])
        mk.op("dve", lambda e: e.memset(onesb[:], 1.0), writes=["onesb"])
        mk.op("dve", lambda e: e.memset(onesf[:], 1.0), writes=["onesf"])
        mk.op("dve", lambda e: e.memset(epst[:], EPS), writes=["epst"])

        with ExitStack() as ph:
            def sbp(name, shape, dt):
                uid[0] += 1
                return ph.enter_context(nc.sbuf_tensor(f"sb{uid[0]}_{name}", list(shape), dt))
            gl = sbp("gl", [80, 128], F32)
            g5l = sbp("g5l", [8, 128], F32)
            sgl = sbp("sgl", [2, 128], F32)
            cl = sbp("cl", [2, D], F32)
            sc = sbp("sc", [2, D], F32)
            scT = sbp("scT", [128, 16, 2], F32)
            awt = [sbp(f"awt{i}", [128, 16, 512], F32) for i in range(2)]
            modtm = sbp("modtm", [2, 6 * D], F32)
            abt = sbp("abt", [2, 6 * D], F32)
            mk.dma("sp", "k0", gl[:], gtab, writes=["gl"])
            mk.dma("sp", "k1", g5l[:], g512, writes=["g5l"])
            mk.dma("sp", "k2", sgl[:], subg_in, writes=["sgl"])
            mk.dma("sp", "k3", cl[:], cvec, writes=["cl"])
            mk.dma("sp", "k4", lamt[:], lvec, writes=["lamt"])
            p0, p0n = PB(0)
            mk.op("pe", lambda e: e.transpose(out=p0[:, 0:80], in_=gl[:], identity=ident[0:80, 0:80]),
                  reads=["gl", "ident"], writes=[p0n])
            mk.op("pe", lambda e: e.transpose(out=p0[:, 80:88], in_=g5l[:], identity=ident[0:8, 0:8]),
                  reads=["g5l", "ident"], writes=[p0n])
            mk.op("pe", lambda e: e.transpose(out=p0[:, 88:90], in_=sgl[:], identity=ident[0:2, 0:2]),
                  reads=["sgl", "ident"], writes=[p0n])
            mk.op("dve", lambda e: e.tensor_copy(out=gT[:], in_=p0[:, 0:80]), reads=[p0n], writes=["gT"])
            mk.op("dve", lambda e: e.tensor_copy(out=g5T[:], in_=p0[:, 80:88]), reads=[p0n], writes=["g5T"])
            mk.op("dve", lambda e: e.tensor_copy(out=subgT[:], in_=p0[:, 88:90]), reads=[p0n], writes=["subgT"])
            lw = sbp("lw", [128, 4], F32)
            mk.op("dve", lambda e: e.tensor_tensor(out=lw[:, 0:1], in0=lamt[:, 0:1], in1=lamt[:, 1:2], op=ALU.mult),
                  reads=["lamt"], writes=["lw0"])
            mk.op("dve", lambda e: e.tensor_tensor(out=lw[:, 1:2], in0=lamt[:, 2:3], in1=lamt[:, 3:4], op=ALU.mult),
                  reads=["lamt"], writes=["lw1"])
            p1, p1n = PB(1)
            mk.op("pe", lambda e: e.matmul(p1[:, 0:2], lhsT=onesf[:], rhs=lw[:, 0:2], start=True, stop=True),
                  reads=["lw0", "lw1", "onesf"], writes=[p1n])
            mk.op("act", lambda e: e.activation(out=lw[:, 2:4], in_=p1[:, 0:2], func=AF.Exp), reads=[p1n], writes=["lw2"])
            mk.op("dve", lambda e: e.tensor_tensor(out=lamt[:, 0:1], in0=lw[:, 3:4], in1=lw[:, 2:3], op=ALU.subtract),
                  reads=["lw2", "lamt"], writes=["lamt0"])
            mk.op("dve", lambda e: e.tensor_scalar(out=lamt[:, 1:2], in0=lamt[:, 0:1], scalar1=-lam_init, scalar2=None,
                                                   op0=ALU.add), reads=["lamt0"], writes=["neglam"])
            mk.op("act", lambda e: e.activation(out=sc[:], in_=cl[:], func=AF.Silu), reads=["cl"], writes=["sc"])
            p2, p2n = PB(2)
            for j in range(16):
                mk.op("pe", lambda e, j=j: e.transpose(out=p2[:, 2 * j:2 * j + 2], in_=sc[:, j * 128:(j + 1) * 128],
                                                       identity=ident[0:2, 0:2]), reads=["sc", "ident"], writes=[p2n])
            mk.op("dve", lambda e: e.tensor_copy(out=scT[:].rearrange("p j m -> p (j m)"), in_=p2[:, 0:32]),
                  reads=[p2n], writes=["scT"])
            for l in range(2):
                mk.dma("sp", "k0", abt[0:1, :], ada_b[l:l + 1, :], writes=["abt"])
                mk.dma("sp", "k1", abt[1:2, :], ada_b[l:l + 1, :], writes=["abt"])
                for nb in range(24):
                    wi = (l * 24 + nb) % 2
                    mk.dma("sp", f"aw{wi}", awt[wi][:],
                           ada_w[l, :, nb * 512:(nb + 1) * 512].rearrange("(k p) n -> p k n", p=128),
                           writes=[f"awt{wi}"])
                    pp, ppn = PB(3 + nb % 2)
                    for k in range(16):
                        mk.op("pe", lambda e, k=k, pp=pp, wi=wi: e.matmul(pp[0:2, :], lhsT=scT[:, k, :], rhs=awt[wi][:, k, :],
                                                                           start=(k == 0), stop=(k == 15)),
                              reads=["scT", f"awt{wi}"], writes=[ppn])
                    mk.op("dve", lambda e, pp=pp, nb=nb: e.tensor_tensor(out=modtm[:, nb * 512:(nb + 1) * 512], in0=pp[0:2, :],
                                                                         in1=abt[:, nb * 512:(nb + 1) * 512], op=ALU.add),
                          reads=[ppn, "abt"], writes=["modtm"])
                p5, p5n = PB(5)
                for c in range(96):
                    mk.op("pe", lambda e, c=c: e.transpose(out=p5[:, 2 * c:2 * c + 2], in_=modtm[:, c * 128:(c + 1) * 128],
                                                           identity=ident[0:2, 0:2]), reads=["modtm", "ident"], writes=[p5n])
                mk.op("dve", lambda e, l=l: e.tensor_copy(out=modT[:, l].rearrange("p c m -> p (c m)"), in_=p5[:, 0:192]),
                      reads=[p5n], writes=["modT"])
                for which in range(2):
                    for m in range(2):
                        ci = 1 + 3 * which
                        gv = which * 2 + l
                        mk.op("dve", lambda e, l=l, which=which, m=m, ci=ci, gv=gv: e.scalar_tensor_tensor(
                            out=AS[:, l, which, m, :], in0=modT[:, l, ci * 16:(ci + 1) * 16, m], scalar=1.0,
                            in1=gT[:, gv * 16:(gv + 1) * 16], op0=ALU.add, op1=ALU.mult),
                            reads=["modT", "gT"], writes=["AS"])
            mk.sync()

        def modv(l, ci, m, j):
            return modT[:, l, ci * 16 + j, m:m + 1]

        with ExitStack() as ph:
            def sbp(name, shape, dt):
                uid[0] += 1
                return ph.enter_context(nc.sbuf_tensor(f"sb{uid[0]}_{name}", list(shape), dt))
            xin = [sbp(f"xin{i}", [128, D], F32) for i in range(2)]
            xo = [sbp(f"xo{i}", [128, 16, 128], F32) for i in range(2)]
            for t in range(NT // 128):
                i = t % 2
                src = x_in[t * 128:(t + 1) * 128, :] if t < 32 else ctx_in[(t - 32) * 128:(t - 31) * 128, :]
                mk.dma("sp", f"xin{i}", xin[i][:], src, writes=[f"xin{i}"])
                for q in range(4):
                    pp, ppn = PB(q)
                    for jj in range(4):
                        j = q * 4 + jj
                        mk.op("pe", lambda e, i=i, j=j, jj=jj, pp=pp: e.transpose(
                            out=pp[:, jj * 128:(jj + 1) * 128], in_=xin[i][:, j * 128:(j + 1) * 128], identity=ident[:]),
                            reads=[f"xin{i}", "ident"], writes=[ppn])
                    eng = "dve" if q % 2 == 0 else "act"
                    if eng == "dve":
                        mk.op("dve", lambda e, i=i, q=q, pp=pp: e.tensor_copy(
                            out=xo[i][:, q * 4:(q + 1) * 4, :].rearrange("p a b -> p (a b)"), in_=pp[:, :]),
                            reads=[ppn], writes=[f"xo{i}"])
                    else:
                        mk.op("act", lambda e, i=i, q=q, pp=pp: e.copy(
                            out=xo[i][:, q * 4:(q + 1) * 4, :].rearrange("p a b -> p (a b)"), in_=pp[:, :]),
                            reads=[ppn], writes=[f"xo{i}"])
                mk.dma("pool", f"xo{i}", XT[:, :, t * 128:(t + 1) * 128].rearrange("j p t -> p j t"), xo[i][:],
                       reads=[f"xo{i}"], writes=[("XT", t // 4)])
            mk.sync()
        if stop_after == 2:
            mk.emit()
            return nc, dbg

        def norm_fm(src, srcn, kcn, T, nfeat, sq, sqn, rs, rsn, Afn, Sfn, dstfn, dstn, tmp=None, ssbank=6):
            mk.op("act", lambda e: e.activation(out=sq[:, 0:kcn, 0:T], in_=src[:, 0:kcn, 0:T], func=AF.Square),
                  reads=[srcn], writes=[sqn])
            ps, psn = PB(ssbank)
            for j in range(kcn):
                mk.op("pe", lambda e, j=j: e.matmul(ps[:, 0:T], lhsT=onesb[:], rhs=sq[:, j, 0:T], start=(j == 0),
                                                    stop=(j == kcn - 1)), reads=[sqn, "onesb"], writes=[psn])
            mk.op("act", lambda e: e.activation(out=rs[:, 0:T], in_=ps[:, 0:T], func=AF.Sqrt, bias=epst[:, 0:1],
                                                scale=1.0 / nfeat), reads=[psn, "epst"], writes=[rsn])
            mk.op("dve", lambda e: e.reciprocal(out=rs[:, 0:T], in_=rs[:, 0:T]), reads=[rsn], writes=[rsn])
            for j in range(kcn):
                if Sfn is None:
                    mk.op("dve", lambda e, j=j: e.scalar_tensor_tensor(out=dstfn(j), in0=src[:, j, 0:T], scalar=Afn(j),
                                                                       in1=rs[:, 0:T], op0=ALU.mult, op1=ALU.mult),
                          reads=[srcn, rsn], writes=[dstn])
                else:
                    tb, tbn = tmp[j % 2]
                    mk.op("dve", lambda e, j=j, tb=tb: e.scalar_tensor_tensor(out=tb[:, 0:T], in0=src[:, j, 0:T], scalar=Afn(j),
                                                                              in1=rs[:, 0:T], op0=ALU.mult, op1=ALU.mult),
                          reads=[srcn, rsn], writes=[tbn])
                    mk.op("act", lambda e, j=j, tb=tb: e.activation(out=dstfn(j), in_=tb[:, 0:T], func=AF.Identity,
                                                                    bias=Sfn(j), scale=1.0), reads=[tbn], writes=[dstn])

        wcount = [0]

        def load_w(wbufs, ws, bi):
            i = wcount[0] % len(wbufs)
            wcount[0] += 1
            kc = ws.kc
            ncol = ws.blocks[bi][1]
            view = wbufs[i][:, 0:kc * ncol].rearrange("p (k n) -> p k n", k=kc)
            mk.dma("sp", f"w{i}", view, ws.t[bi], reads=[(ws.name, bi)], writes=[f"wbuf{i}"])
            return view, f"wbuf{i}"

        ppc = [0]

        def proj_fm(wv, wn, act, actn, kcn, ncols, T, epi, banks=(0, 1, 2, 3)):
            for c in range((ncols + 127) // 128):
                M = min(128, ncols - c * 128)
                b = banks[ppc[0] % len(banks)]
                ppc[0] += 1
                ps, psn = PB(b)
                for k in range(kcn):
                    mk.op("pe", lambda e, k=k, c=c, M=M, ps=ps: e.matmul(ps[0:M, 0:T], lhsT=wv[:, k, c * 128:c * 128 + M],
                                                                         rhs=act[:, k, 0:T], start=(k == 0), stop=(k == kcn - 1)),
                          reads=[wn, actn], writes=[psn])
                epi(c, ps, psn, M)

        def proj_tm(wv, wn, act, actn, kcn, ncols, T, epi, banks=(0, 1, 2, 3)):
            for t in range(T // 128):
                b = banks[ppc[0] % len(banks)]
                ppc[0] += 1
                ps, psn = PB(b)
                for k in range(kcn):
                    mk.op("pe", lambda e, k=k, t=t, ps=ps: e.matmul(ps[:, 0:ncols], lhsT=act[:, k, t * 128:(t + 1) * 128],
                                                                    rhs=wv[:, k, 0:ncols], start=(k == 0), stop=(k == kcn - 1)),
                          reads=[wn, actn], writes=[psn])
                epi(t, ps, psn)

        stc = [0]

        def rope_fm(ps, psn, M, T, t0, cosT, sinT, permT, dst, dstn, tmps):
            i = stc[0] % 2
            stc[0] += 1
            xs, xsn = tmps[0][i]
            t1, t1n = tmps[1][i]
            t2, t2n = tmps[2][i]
            mk.op("act", lambda e: e.copy(out=xs[0:M, 0:T], in_=ps[0:M, 0:T]), reads=[psn], writes=[xsn])
            pq, pqn = PB(4 + i)
            mk.op("pe", lambda e: e.matmul(pq[0:M, 0:T], lhsT=permT[0:M, 0:M], rhs=xs[0:M, 0:T], start=True, stop=True),
                  reads=[xsn, "perm"], writes=[pqn])
            mk.op("pool", lambda e: e.tensor_tensor(out=t1[0:M, 0:T], in0=xs[0:M, 0:T], in1=cosT[0:M, t0:t0 + T], op=ALU.mult),
                  reads=[xsn, "ropetab"], writes=[t1n])
            mk.op("dve", lambda e: e.tensor_tensor(out=t2[0:M, 0:T], in0=pq[0:M, 0:T], in1=sinT[0:M, t0:t0 + T], op=ALU.mult),
                  reads=[pqn, "ropetab"], writes=[t2n])
            mk.op("pool", lambda e: e.tensor_tensor(out=dst, in0=t1[0:M, 0:T], in1=t2[0:M, 0:T], op=ALU.add),
                  reads=[t1n, t2n], writes=[dstn])

        def store(dst, src, srcn, dstn):
            mk.dma("pool", srcn if isinstance(srcn, str) else "st", dst, src, reads=[srcn], writes=[dstn])

        with ExitStack() as ph:
            def sbp(name, shape, dt):
                uid[0] += 1
                return ph.enter_context(nc.sbuf_tensor(f"sb{uid[0]}_{name}", list(shape), dt))
            xg = sbp("xg", [128, 16, 512], F32)
            sq = sbp("sq", [128, 16, 512], BF16)
            hT = sbp("hT", [128, 16, 512], BF16)
            rs = sbp("rs", [128, 512], F32)
            tmpa = [(sbp(f"tmpa{i}", [128, 512], F32), f"tmpa{i}") for i in range(2)]
            wbufs = [sbp(f"wbuf{i}", [128, 8192], BF16) for i in range(3)]
            stb = [sbp(f"stb{i}", [128, 512], BF16) for i in range(4)]
            stf = [sbp(f"stf{i}", [128, 512], F32) for i in range(4)]
            cnt = [0]
            for gi, (t0, T) in enumerate(GROUPS):
                m = 0 if t0 < NLAT else 1
                mk.dma("sp", "xg", xg[:, :, 0:T], XT[:, :, t0:t0 + T].rearrange("j p t -> p j t"), writes=["xg"])
                norm_fm(xg, "xg", 16, T, D, sq, "sq", rs, "rs",
                        lambda j, m=m: AS[:, 0, 0, m, j:j + 1], lambda j, m=m: modv(0, 0, m, j),
                        lambda j, T=T: hT[:, j, 0:T], "hT", tmp=tmpa)
                for bi, (c0, ncol) in enumerate(ws_in0.blocks):
                    wv, wn = load_w(wbufs, ws_in0, bi)
                    if 2560 <= c0 < 3584:
                        def epi_v(t, ps, psn, c0=c0, t0=t0):
                            i = cnt[0] % 4; cnt[0] += 1
                            mk.op("act", lambda e: e.copy(out=stb[i][:, :], in_=ps[:, :]), reads=[psn], writes=[f"stb{i}"])
                            store(VA[t0 + t * 128:t0 + (t + 1) * 128, c0 - 2560:c0 - 2560 + 512], stb[i][:, :], f"stb{i}", ("VA", t0))
                        proj_tm(wv, wn, hT, "hT", 16, ncol, T, epi_v)
                        continue

                    def epi(c, ps, psn, M, c0=c0, t0=t0, T=T):
                        col = c0 + c * 128
                        i = cnt[0] % 4; cnt[0] += 1
                        if col < 1024:
                            mk.op("act", lambda e: e.mul(out=stb[i][:, 0:T], in_=ps[:, 0:T], mul=128 ** -0.5), reads=[psn], writes=[f"stb{i}"])
                            store(QA[col // 128, :, t0:t0 + T], stb[i][:, 0:T], f"stb{i}", ("QA", t0))
                        elif col < 1536:
                            mk.op("dve", lambda e: e.tensor_copy(out=stf[i][:, 0:T], in_=ps[:, 0:T]), reads=[psn], writes=[f"stf{i}"])
                            store(CQ[(col - 1024) // 128, :, t0:t0 + T], stf[i][:, 0:T], f"stf{i}", ("CQ", t0))
                        elif col < 2560:
                            mk.op("dve", lambda e: e.tensor_copy(out=stb[i][:, 0:T], in_=ps[:, 0:T]), reads=[psn], writes=[f"stb{i}"])
                            store(KA[(col - 1536) // 128, :, t0:t0 + T], stb[i][:, 0:T], f"stb{i}", ("KA", t0))
                        elif col < 4096:
                            mk.op("dve", lambda e: e.tensor_copy(out=stf[i][:, 0:T], in_=ps[:, 0:T]), reads=[psn], writes=[f"stf{i}"])
                            store(CKV[(col - 3584) // 128, :, t0:t0 + T], stf[i][:, 0:T], f"stf{i}", ("CKV", t0))
                        else:
                            mk.op("dve", lambda e: e.tensor_copy(out=stf[i][0:64, 0:T], in_=ps[0:64, 0:T]), reads=[psn], writes=[f"stf{i}"])
                            store(KR[:, t0:t0 + T], stf[i][0:64, 0:T], f"stf{i}", ("KR", t0))
                    proj_fm(wv, wn, hT, "hT", 16, ncol, T, epi)
            mk.sync()
        if stop_after == 3:
            mk.emit()
            return nc, dbg

        SB = (0, 1, 6)

        def attn_chunk(qparts, kparts, rnames, vaug, vn, dv, ktiles, nq, scale, pTb, bias=None, acc0=2):
            nu = nq // 128
            accs = [PB(acc0 + u) for u in range(nu)]
            nk = len(ktiles)
            np_ = len(qparts)

            def S(ti):
                t = ktiles[ti]
                ps, psn = PB(SB[ti % 3])
                hasb = bias is not None and bias(ti) is not None
                for pi in range(np_):
                    last = (pi == np_ - 1) and not hasb
                    mk.op("pe", lambda e, pi=pi, t=t, ps=ps, last=last: e.matmul(ps[:, 0:nq], lhsT=kparts[pi](t), rhs=qparts[pi],
                                                                                 start=(pi == 0), stop=last),
                          reads=rnames, writes=[psn])
                if hasb:
                    mk.op("pe", lambda e, ti=ti, ps=ps: e.matmul(ps[:, 0:nq], lhsT=bias(ti), rhs=identb[:, 0:nq], start=False, stop=True),
                          reads=["bias", "identb"], writes=[psn])
                pT, pTn = pTb[ti % 3]
                mk.op("act", lambda e, ps=ps, pT=pT: e.activation(out=pT[:, 0:nq], in_=ps[:, 0:nq], func=AF.Exp, scale=scale),
                      reads=[psn], writes=[pTn])

            def PV(ti):
                t = ktiles[ti]
                pT, pTn = pTb[ti % 3]
                for u in range(nu):
                    acc, accn = accs[u]
                    mk.op("pe", lambda e, u=u, t=t, acc=acc, pT=pT, ti=ti: e.matmul(acc[:, 0:dv + 1], lhsT=pT[:, u * 128:(u + 1) * 128],
                                                                                    rhs=vaug(t), start=(ti == 0), stop=(ti == nk - 1)),
                          reads=[pTn, vn], writes=[accn])

            for ti in range(min(2, nk)):
                S(ti)
            for ti in range(nk):
                if ti + 2 < nk:
                    S(ti + 2)
                PV(ti)
            return accs

        def finish_simple(accs, nq, dv, rec, ob, obn, oT, oTn, dst_chunk_ap, dstn):
            nu = nq // 128
            for u, (acc, accn) in enumerate(accs):
                mk.op("dve", lambda e, u=u, acc=acc: e.reciprocal(out=rec[:, u:u + 1], in_=acc[:, dv:dv + 1]), reads=[accn], writes=[("rec", u)])
                mk.op("dve", lambda e, u=u, acc=acc: e.tensor_scalar(out=ob[:, u, :], in0=acc[:, 0:dv], scalar1=rec[:, u:u + 1], scalar2=None,
                                                                     op0=ALU.mult), reads=[accn, ("rec", u)], writes=[(obn, u)])
                mk.op("pe", lambda e, u=u: e.transpose(out=PTt[:, u * 128:(u + 1) * 128], in_=ob[:, u, :], identity=identb[:]),
                      reads=[(obn, u), "identb"], writes=["ptb"])
            mk.op("act", lambda e: e.copy(out=oT[:, 0:nq], in_=PTt[:, 0:nq]), reads=["ptb"], writes=[oTn])
            store(dst_chunk_ap, oT[:, 0:nq], oTn, dstn)

        with ExitStack() as ph:
            def sbp(name, shape, dt):
                uid[0] += 1
                return ph.enter_context(nc.sbuf_tensor(f"sb{uid[0]}_{name}", list(shape), dt))
            ckvn = sbp("ckvn", [128, 4, NT], BF16)
            cqn = sbp("cqn", [128, 4, NT], BF16)
            krT = sbp("krT", [64, NT], BF16)
            cosm = sbp("cosm", [64, NT], F32)
            sinm = sbp("sinm", [64, NT], F32)
            perm64 = sbp("perm64", [64, 64], F32)
            cg = sbp("cg", [128, 4, 512], F32)
            sq = sbp("sq4", [128, 4, 512], BF16)
            rs = sbp("rs4", [128, 512], F32)
            tm = [[(sbp(f"rt{a}{i}", [128, 512], F32), f"rt{a}{i}") for i in range(2)] for a in range(3)]
            wuqs = [sbp(f"wuq{i}", [128, 4, 192], BF16) for i in range(2)]
            wukvs = [sbp(f"wukv{i}", [128, 4, 256], BF16) for i in range(2)]
            kT = sbp("kT", [128, NT], BF16)
            qT = sbp("qT", [128, NT], BF16)
            qrT = sbp("qrT", [64, NT], BF16)
            vaug = sbp("vaug", [128, 34, 129], BF16)
            pTb = [(sbp(f"pT{i}", [128, 512], BF16), f"pT{i}") for i in range(3)]
            rec = sbp("rec", [128, 8], F32)
            ob = sbp("ob", [128, 4, 128], BF16)
            oT = [sbp(f"oT{i}", [128, 512], BF16) for i in range(2)]
            bt = sbp("bt", [128, NBT, 128], BF16)
            mk.dma("sp", "k0", cosm[:], cosm_in, writes=["ropetab"])
            mk.dma("sp", "k1", sinm[:], sinm_in, writes=["ropetab"])
            mk.dma("sp", "k2", perm64[:], perm64_in, writes=["perm"])
            mk.op("dve", lambda e: e.memset(vaug[:, :, 128:129], 1.0), writes=["vaug"])
            for gi, (t0, T) in enumerate(GROUPS):
                for (src_d, dstt, dn, goff) in ((CKV, ckvn, "ckvn", 4), (CQ, cqn, "cqn", 0)):
                    mk.dma("sp", "cg", cg[:, :, 0:T], src_d[:, :, t0:t0 + T].rearrange("j p t -> p j t"), writes=["cg"])
                    norm_fm(cg, "cg", 4, T, 512, sq, "sq4", rs, "rs4", lambda j, goff=goff: g5T[:, goff + j:goff + j + 1], None,
                            lambda j, dstt=dstt, t0=t0, T=T: dstt[:, j, t0:t0 + T], dn)
                mk.dma("sp", "cg", cg[0:64, 0, 0:T], KR[:, t0:t0 + T], writes=["cg"])
                pq, pqn = PB(4 + gi % 2)
                mk.op("pe", lambda e, pq=pq, T=T: e.matmul(pq[0:64, 0:T], lhsT=perm64[:, :], rhs=cg[0:64, 0, 0:T], start=True, stop=True),
                      reads=["cg", "perm"], writes=[pqn])
                t1, t1n = tm[1][gi % 2]
                t2, t2n = tm[2][gi % 2]
                mk.op("pool", lambda e, t1=t1, t0=t0, T=T: e.tensor_tensor(out=t1[0:64, 0:T], in0=cg[0:64, 0, 0:T], in1=cosm[:, t0:t0 + T], op=ALU.mult),
                      reads=["cg", "ropetab"], writes=[t1n])
                mk.op("dve", lambda e, t2=t2, pq=pq, t0=t0, T=T: e.tensor_tensor(out=t2[0:64, 0:T], in0=pq[0:64, 0:T], in1=sinm[:, t0:t0 + T], op=ALU.mult),
                      reads=[pqn, "ropetab"], writes=[t2n])
                mk.op("pool", lambda e, t1=t1, t2=t2, t0=t0, T=T: e.tensor_tensor(out=krT[:, t0:t0 + T], in0=t1[0:64, 0:T], in1=t2[0:64, 0:T], op=ALU.add),
                      reads=[t1n, t2n], writes=["krT"])
            mla_scale = 192 ** -0.5
            for h in range(8):
                wuq = wuqs[h % 2]; wukv = wukvs[h % 2]
                wuqn = f"wuq{h % 2}"; wukvn = f"wukv{h % 2}"
                mk.dma("sp", wuqn, wuq[:], ws_uq.t[0][:, :, h * 192:(h + 1) * 192], reads=[("w_uq", 0)], writes=[wuqn])
                mk.dma("sp", wukvn, wukv[:], ws_ukv.t[0][:, :, h * 256:(h + 1) * 256], reads=[("w_ukv", 0)], writes=[wukvn])
                for gi, (t0, T) in enumerate(GROUPS):
                    def epi_k(c, ps, psn, M, t0=t0, T=T):
                        mk.op("dve", lambda e: e.tensor_copy(out=kT[:, t0:t0 + T], in_=ps[:, 0:T]), reads=[psn], writes=["kT"])
                    proj_fm(wukv[:, :, 0:128], wukvn, ckvn[:, :, t0:t0 + T], "ckvn", 4, 128, T, epi_k)
                    def epi_q(c, ps, psn, M, t0=t0, T=T):
                        mk.op("act", lambda e: e.copy(out=qT[:, t0:t0 + T], in_=ps[:, 0:T]), reads=[psn], writes=["qT"])
                    proj_fm(wuq[:, :, 0:128], wuqn, cqn[:, :, t0:t0 + T], "cqn", 4, 128, T, epi_q)
                    def epi_qr(c, ps, psn, M, t0=t0, T=T):
                        rope_fm(ps, psn, 64, T, t0, cosm, sinm, perm64, qrT[:, t0:t0 + T], "qrT", tm)
                    proj_fm(wuq[:, :, 128:192], wuqn, cqn[:, :, t0:t0 + T], "cqn", 4, 64, T, epi_qr)
                    def epi_vv(t, ps, psn, t0=t0):
                        mk.op("dve", lambda e: e.tensor_copy(out=vaug[:, t0 // 128 + t, 0:128], in_=ps[:, 0:128]), reads=[psn], writes=["vaug"])
                    proj_tm(wukv[:, :, 128:256], wukvn, ckvn[:, :, t0:t0 + T], "ckvn", 4, 128, T, epi_vv)
                for gi, (t0, T) in enumerate(GROUPS):
                    ktiles = list(range(34)) if t0 < NLAT else [32, 33]
                    accs = attn_chunk([qT[:, t0:t0 + T], qrT[:, t0:t0 + T]],
                                      [lambda t: kT[:, t * 128:(t + 1) * 128], lambda t: krT[:, t * 128:(t + 1) * 128]],
                                      ["qT", "qrT", "kT", "krT"], lambda t: vaug[:, t, :], "vaug", 128, ktiles, T, mla_scale, pTb)
                    finish_simple(accs, T, 128, rec, ob, "ob", oT[gi % 2], f"oT{gi % 2}", OT[8 + h, :, t0:t0 + T], ("OT", 8 + h))
            for h in range(8):
                mk.dma("sp", "k0", qT[:, :], QA[h], writes=["qT"])
                mk.dma("sp", "k1", kT[:, :], KA[h], writes=["kT"])
                mk.dma("sp", "k2", vaug[:, :, 0:128], VA[:, h * 128:(h + 1) * 128].rearrange("(t p) c -> p t c", p=128), writes=["vaug"])
                mk.dma("pool", "btl", bt[:], btab[h].rearrange("t q k -> q t k"), writes=["bias"])
                for g in range(9):
                    nq = 512 if g < 8 else 256
                    nu = nq // 128
                    groups = []
                    for u in range(nu):
                        jq = g * 4 + u
                        if jq < 32:
                            lst = NA_PLAN[jq] + [(32, None), (33, None)]
                        else:
                            lst = [(32, None), (33, None)]
                        for s0 in range(0, len(lst), 4):
                            groups.append((u, jq, lst[s0:s0 + 4], s0 == 0, s0 + 4 >= len(lst)))
                    accs_all = [PB(2 + u) for u in range(nu)]

                    def S_na(n):
                        u, jq, items, first, last = groups[n]
                        ps, psn = PB(SB[n % 3])
                        for i, (t, bi) in enumerate(items):
                            mk.op("pe", lambda e, i=i, t=t, jq=jq, ps=ps, bi=bi: e.matmul(
                                ps[:, i * 128:(i + 1) * 128], lhsT=kT[:, t * 128:(t + 1) * 128], rhs=qT[:, jq * 128:(jq + 1) * 128],
                                start=True, stop=(bi is None)), reads=["qT", "kT"], writes=[psn])
                            if bi is not None:
                                mk.op("pe", lambda e, i=i, ps=ps, bi=bi: e.matmul(ps[:, i * 128:(i + 1) * 128], lhsT=bt[:, bi, :], rhs=identb[:, :],
                                                                                 start=False, stop=True), reads=["bias", "identb"], writes=[psn])
                        pT, pTn = pTb[n % 3]
                        w = len(items) * 128
                        mk.op("act", lambda e, ps=ps, pT=pT, w=w: e.activation(out=pT[:, 0:w], in_=ps[:, 0:w], func=AF.Exp, scale=1.0),
                              reads=[psn], writes=[pTn])

                    def PV_na(n):
                        u, jq, items, first, last = groups[n]
                        pT, pTn = pTb[n % 3]
                        acc, accn = accs_all[u]
                        for i, (t, bi) in enumerate(items):
                            mk.op("pe", lambda e, i=i, t=t, acc=acc, pT=pT, st=(first and i == 0), sp_=(last and i == len(items) - 1): e.matmul(
                                acc[:, 0:129], lhsT=pT[:, i * 128:(i + 1) * 128], rhs=vaug[:, t, :], start=st, stop=sp_),
                                reads=[pTn, "vaug"], writes=[accn])

                    for n in range(min(2, len(groups))):
                        S_na(n)
                    for n in range(len(groups)):
                        if n + 2 < len(groups):
                            S_na(n + 2)
                        PV_na(n)
                    finish_simple(accs_all, nq, 128, rec, ob, "ob", oT[g % 2], f"oT{g % 2}", OT[h, :, g * 512:g * 512 + nq], ("OT", h))
            mk.sync()
        if stop_after == 5:
            mk.emit()
            return nc, dbg

        def post_phase(l, ws_out, final):
            with ExitStack() as ph:
                def sbp(name, shape, dt):
                    uid[0] += 1
                    return ph.enter_context(nc.sbuf_tensor(f"sb{uid[0]}_{name}", list(shape), dt))
                og = sbp("og", [128, 16, 512], BF16)
                xg = sbp("xg", [128, 16, 512], F32)
                sq = sbp("sq", [128, 16, 512], BF16)
                hT = sbp("hT", [128, 16, 512], BF16)
                mid = sbp("mid", [128, 32, 512], BF16)
                rs = sbp("rs", [128, 512], F32)
                tmpa = [(sbp(f"tmpa{i}", [128, 512], F32), f"tmpa{i}") for i in range(2)]
                wbufs = [sbp(f"wbuf{i}", [128, 8192], BF16) for i in range(3)]
                yo = [sbp(f"yo{i}", [128, 4, 128], F32) for i in range(2)]
                groups = GROUPS[:8] if final else GROUPS
                for gi, (t0, T) in enumerate(groups):
                    m = 0 if t0 < NLAT else 1
                    nsrc = NLAT if l == 1 else NT
                    mk.dma("sp", "og", og[:, :, 0:T], OT[:, :, t0:t0 + T].rearrange("j p t -> p j t"), writes=["og"])
                    mk.dma("sp", "xg", xg[:, :, 0:T], XT[:, :, t0:t0 + T].rearrange("j p t -> p j t"), writes=["xg"])
                    for bi, (c0, ncol) in enumerate(ws_out.blocks):
                        wv, wn = load_w(wbufs, ws_out, bi)
                        def epi(c, ps, psn, M, c0=c0, T=T, m=m):
                            j = c0 // 128 + c
                            mk.op("dve", lambda e: e.scalar_tensor_tensor(out=xg[:, j, 0:T], in0=ps[:, 0:T], scalar=modv(l, 2, m, j),
                                                                          in1=xg[:, j, 0:T], op0=ALU.mult, op1=ALU.add),
                                  reads=[psn, "xg"], writes=["xg"])
                        proj_fm(wv, wn, og, "og", 16, ncol, T, epi)
                    norm_fm(xg, "xg", 16, T, D, sq, "sq", rs, "rs",
                            lambda j, m=m: AS[:, l, 1, m, j:j + 1], lambda j, m=m: modv(l, 3, m, j),
                            lambda j, T=T: hT[:, j, 0:T], "hT", tmp=tmpa)
                    for half in range(2):
                        for hb in range(8):
                            wv, wn = load_w(wbufs, ws_w1[l], half * 8 + hb)
                            def epi1(c, ps, psn, M, hb=hb, T=T):
                                kk = hb * 4 + c
                                tb, tbn = tmpa[kk % 2]
                                mk.op("act", lambda e: e.activation(out=tb[:, 0:T], in_=ps[:, 0:T], func=AF.Relu), reads=[psn], writes=[tbn])
                                mk.op("pool", lambda e: e.tensor_tensor(out=mid[:, kk, 0:T], in0=tb[:, 0:T], in1=tb[:, 0:T], op=ALU.mult),
                                      reads=[tbn], writes=["mid"])
                            proj_fm(wv, wn, hT, "hT", 16, 512, T, epi1)
                        for nb in range(16):
                            i = wcount[0] % len(wbufs)
                            wcount[0] += 1
                            view = wbufs[i][:, 0:32 * 128].rearrange("p (k n) -> p k n", k=32)
                            mk.dma("sp", f"w{i}", view, ws_w2[l].t[nb][:, half * 32:(half + 1) * 32, :], reads=[(ws_w2[l].name, nb)],
                                   writes=[f"wbuf{i}"])
                            def epi2(c, ps, psn, M, nb=nb, T=T, m=m):
                                mk.op("dve", lambda e: e.scalar_tensor_tensor(out=xg[:, nb, 0:T], in0=ps[:, 0:T], scalar=modv(l, 5, m, nb),
                                                                              in1=xg[:, nb, 0:T], op0=ALU.mult, op1=ALU.add),
                                      reads=[psn, "xg"], writes=["xg"])
                            proj_fm(view, f"wbuf{i}", mid, "mid", 32, 128, T, epi2)
                    if not final:
                        mk.dma("pool", "xgs", XT[:, :, t0:t0 + T].rearrange("j p t -> p j t"), xg[:, :, 0:T], reads=["xg"], writes=[("XT", gi)])
                    else:
                        xn = sbp(f"xn{gi}", [1, 1], F32) if False else None
                        norm_fm(xg, "xg", 16, T, D, sq, "sq", rs, "rs", lambda j: gT[:, 64 + j:65 + j], None,
                                lambda j, T=T: xg[:, j, 0:T], "xg")
                        for tt in range(T // 128):
                            for q in range(4):
                                yi = (tt * 4 + q) % 2
                                pp, ppn = PB(q)
                                for jj in range(4):
                                    j = q * 4 + jj
                                    mk.op("pe", lambda e, j=j, jj=jj, tt=tt, pp=pp: e.transpose(
                                        out=pp[:, jj * 128:(jj + 1) * 128], in_=xg[:, j, tt * 128:(tt + 1) * 128], identity=ident[:]),
                                        reads=["xg", "ident"], writes=[ppn])
                                if q % 2 == 0:
                                    mk.op("dve", lambda e, yi=yi, pp=pp: e.tensor_copy(out=yo[yi][:].rearrange("p a b -> p (a b)"), in_=pp[:, :]),
                                          reads=[ppn], writes=[f"yo{yi}"])
                                else:
                                    mk.op("act", lambda e, yi=yi, pp=pp: e.copy(out=yo[yi][:].rearrange("p a b -> p (a b)"), in_=pp[:, :]),
                                          reads=[ppn], writes=[f"yo{yi}"])
                                mk.dma("pool", f"yo{yi}", out_d[t0 + tt * 128:t0 + (tt + 1) * 128, q * 512:(q + 1) * 512],
                                       yo[yi][:].rearrange("p a b -> p (a b)"), reads=[f"yo{yi}"])
                mk.sync()

        post_phase(0, ws_out0, False)
        if stop_after == 6:
            mk.emit()
            return nc, dbg

        with ExitStack() as ph:
            def sbp(name, shape, dt):
                uid[0] += 1
                return ph.enter_context(nc.sbuf_tensor(f"sb{uid[0]}_{name}", list(shape), dt))
            xg = sbp("xg", [128, 16, 512], F32)
            sq = sbp("sq", [128, 16, 512], BF16)
            hT = sbp("hT", [128, 16, 512], BF16)
            rs = sbp("rs", [128, 512], F32)
            tmpa = [(sbp(f"tmpa{i}", [128, 512], F32), f"tmpa{i}") for i in range(2)]
            tm = [[(sbp(f"rt{a}{i}", [128, 512], F32), f"rt{a}{i}") for i in range(2)] for a in range(3)]
            wbufs = [sbp(f"wbuf{i}", [128, 8192], BF16) for i in range(3)]
            stb = [sbp(f"stb{i}", [128, 512], BF16) for i in range(4)]
            cosd = sbp("cosd", [128, NT], F32)
            sind = sbp("sind", [128, NT], F32)
            perm128 = sbp("perm128", [128, 128], F32)
            mk.dma("sp", "k0", cosd[:], cosd_in, writes=["ropetab"])
            mk.dma("sp", "k1", sind[:], sind_in, writes=["ropetab"])
            mk.dma("sp", "k2", perm128[:], perm128_in, writes=["perm"])
            cnt = [0]
            for gi, (t0, T) in enumerate(GROUPS):
                m = 0 if t0 < NLAT else 1
                mk.dma("sp", "xg", xg[:, :, 0:T], XT[:, :, t0:t0 + T].rearrange("j p t -> p j t"), writes=["xg"])
                norm_fm(xg, "xg", 16, T, D, sq, "sq", rs, "rs",
                        lambda j, m=m: AS[:, 1, 0, m, j:j + 1], lambda j, m=m: modv(1, 0, m, j),
                        lambda j, T=T: hT[:, j, 0:T], "hT", tmp=tmpa)
                for bi, (c0, ncol) in enumerate(ws_in1.blocks):
                    if c0 < 2048 and m == 1:
                        continue
                    wv, wn = load_w(wbufs, ws_in1, bi)
                    if c0 >= 4096:
                        def epi_v(t, ps, psn, c0=c0, t0=t0):
                            i = cnt[0] % 4; cnt[0] += 1
                            mk.op("act", lambda e: e.copy(out=stb[i][:, :], in_=ps[:, :]), reads=[psn], writes=[f"stb{i}"])
                            store(VD[t0 + t * 128:t0 + (t + 1) * 128, c0 - 4096:c0 - 4096 + 512], stb[i][:, :], f"stb{i}", ("VD", t0))
                        proj_tm(wv, wn, hT, "hT", 16, ncol, T, epi_v)
                        continue

                    def epi(c, ps, psn, M, c0=c0, t0=t0, T=T):
                        col = c0 + c * 128
                        i = cnt[0] % 4; cnt[0] += 1
                        rope_fm(ps, psn, 128, T, t0, cosd, sind, perm128, stb[i][:, 0:T], f"stb{i}", tm)
                        if col < 2048:
                            store(QD[col // 128, :, t0:t0 + T], stb[i][:, 0:T], f"stb{i}", ("QD", t0))
                        else:
                            store(KD[(col - 2048) // 128, :, t0:t0 + T], stb[i][:, 0:T], f"stb{i}", ("KD", t0))
                    proj_fm(wv, wn, hT, "hT", 16, ncol, T, epi, banks=(0, 1, 2, 3))
            mk.sync()
        if stop_after == 7:
            mk.emit()
            return nc, dbg

        with ExitStack() as ph:
            def sbp(name, shape, dt):
                uid[0] += 1
                return ph.enter_context(nc.sbuf_tensor(f"sb{uid[0]}_{name}", list(shape), dt))
            sets = []
            for si in range(2):
                sets.append(dict(q1=sbp(f"q1{si}", [128, NLAT], BF16), q2=sbp(f"q2{si}", [128, NLAT], BF16),
                                 k1=sbp(f"k1{si}", [128, NT], BF16), k2=sbp(f"k2{si}", [128, NT], BF16),
                                 v=sbp(f"vaugd{si}", [128, 34, 257], BF16)))
            pTb = [(sbp(f"pT{i}", [128, 512], BF16), f"pT{i}") for i in range(3)]
            rec = sbp("rec", [128, 8], F32)
            o1 = sbp("o1", [128, 4, 256], F32)
            oo = sbp("oo", [128, 4, 256], F32)
            junk = sbp("junk", [128, 256], F32)
            ssq = sbp("ssq", [128, 8], F32)
            ob = sbp("obd", [128, 4, 256], BF16)
            oT = [sbp(f"oT{i}", [128, 512], BF16) for i in range(2)]
            for si in range(2):
                mk.op("dve", lambda e, si=si: e.memset(sets[si]["v"][:, :, 256:257], 1.0), writes=[f"vaug{si}"])

            def load_head(h):
                si = h % 2
                S_ = sets[si]
                key = f"qa{si}"
                mk.dma("sp", key + "a", S_["q1"][:, :], QD[2 * h], writes=[f"q1{si}"])
                mk.dma("sp", key + "b", S_["k1"][:, :], KD[2 * h], writes=[f"k1{si}"])
                mk.dma("sp", key + "c", S_["v"][:, :, 0:256], VD[:, h * 256:(h + 1) * 256].rearrange("(t p) c -> p t c", p=128), writes=[f"vaug{si}"])
                mk.dma("sp", key + "d", S_["q2"][:, :], QD[2 * h + 1], writes=[f"q2{si}"])
                mk.dma("sp", key + "e", S_["k2"][:, :], KD[2 * h + 1], writes=[f"k2{si}"])

            load_head(0)
            for h in range(8):
                if h + 1 < 8:
                    load_head(h + 1)
                si = h % 2
                q1 = sets[si]["q1"]; q2 = sets[si]["q2"]; k1 = sets[si]["k1"]; k2 = sets[si]["k2"]; vaug = sets[si]["v"]
                q1n, q2n, k1n, k2n, vgn = f"q1{si}", f"q2{si}", f"k1{si}", f"k2{si}", f"vaug{si}"
                for g in range(8):
                    t0 = g * 512
                    kt = list(range(34))
                    accs = attn_chunk([q1[:, t0:t0 + 512]], [lambda t, k1=k1: k1[:, t * 128:(t + 1) * 128]], [q1n, k1n],
                                      lambda t, vaug=vaug: vaug[:, t, :], vgn, 256, kt, 512, 128 ** -0.5, pTb)
                    for u, (acc, accn) in enumerate(accs):
                        mk.op("dve", lambda e, u=u, acc=acc: e.reciprocal(out=rec[:, u:u + 1], in_=acc[:, 256:257]), reads=[accn], writes=[("rec", u)])
                        mk.op("dve", lambda e, u=u, acc=acc: e.tensor_scalar(out=o1[:, u, :], in0=acc[:, 0:256], scalar1=rec[:, u:u + 1], scalar2=None,
                                                                             op0=ALU.mult), reads=[accn, ("rec", u)], writes=[("o1", u)])
                    accs = attn_chunk([q2[:, t0:t0 + 512]], [lambda t, k2=k2: k2[:, t * 128:(t + 1) * 128]], [q2n, k2n],
                                      lambda t, vaug=vaug: vaug[:, t, :], vgn, 256, kt, 512, 128 ** -0.5, pTb)
                    for u, (acc, accn) in enumerate(accs):
                        mk.op("dve", lambda e, u=u, acc=acc: e.reciprocal(out=rec[:, 4 + u:5 + u], in_=acc[:, 256:257]), reads=[accn], writes=[("rec2", u)])
                        mk.op("dve", lambda e, u=u: e.tensor_scalar(out=rec[:, 4 + u:5 + u], in0=rec[:, 4 + u:5 + u], scalar1=lamt[:, 1:2], scalar2=None,
                                                                    op0=ALU.mult), reads=[("rec2", u)], writes=[("rec2", u)])
                        mk.op("dve", lambda e, u=u, acc=acc: e.scalar_tensor_tensor(out=oo[:, u, :], in0=acc[:, 0:256], scalar=rec[:, 4 + u:5 + u],
                                                                                     in1=o1[:, u, :], op0=ALU.mult, op1=ALU.add),
                              reads=[accn, ("rec2", u), ("o1", u)], writes=[("oo", u)])
                        mk.op("act", lambda e, u=u: e.activation(out=junk[:, :], in_=oo[:, u, :], func=AF.Square, accum_out=ssq[:, u:u + 1]),
                              reads=[("oo", u)], writes=["junk", ("ssq", u)])
                        mk.op("act", lambda e, u=u: e.activation(out=ssq[:, 4 + u:5 + u], in_=ssq[:, u:u + 1], func=AF.Sqrt, bias=epst[:, 0:1],
                                                                 scale=1.0 / 256), reads=[("ssq", u)], writes=[("ssr", u)])
                        mk.op("dve", lambda e, u=u: e.reciprocal(out=ssq[:, 4 + u:5 + u], in_=ssq[:, 4 + u:5 + u]), reads=[("ssr", u)], writes=[("ssr", u)])
                        mk.op("dve", lambda e, u=u: e.tensor_scalar(out=ob[:, u, :], in0=oo[:, u, :], scalar1=ssq[:, 4 + u:5 + u], scalar2=1.0 - lam_init,
                                                                    op0=ALU.mult, op1=ALU.mult), reads=[("oo", u), ("ssr", u)], writes=[("obd", u)])
                        for c in range(2):
                            mk.op("pe", lambda e, u=u, c=c: e.transpose(out=PTt[:, c * 512 + u * 128:c * 512 + (u + 1) * 128],
                                                                        in_=ob[:, u, c * 128:(c + 1) * 128], identity=identb[:]),
                                  reads=[("obd", u), "identb"], writes=["ptb"])
                    for c in range(2):
                        mk.op("act", lambda e, c=c: e.activation(out=oT[c][:, :], in_=PTt[:, c * 512:(c + 1) * 512], func=AF.Copy,
                                                                 scale=subgT[:, c:c + 1]), reads=["ptb"], writes=[f"oT{c}"])
                        store(OT[2 * h + c, :, t0:t0 + 512], oT[c][:, :], f"oT{c}", ("OT", 2 * h + c))
            mk.sync()
        if stop_after == 8:
            mk.emit()
            return nc, dbg
        post_phase(1, ws_out1, True)
        mk.emit()
    return nc, dbg


def _host_inputs(inputs, b):
    f = np.float32
    x = np.ascontiguousarray(inputs["x"][b], dtype=f)
    ctx = np.ascontiguousarray(inputs["ctx"][b], dtype=f)
    cvec = np.stack([inputs["c"][b], inputs["c_ctx"]]).astype(f)
    gt = np.concatenate([inputs["norm1_g"][0], inputs["norm1_g"][1], inputs["norm2_g"][0], inputs["norm2_g"][1],
                         inputs["final_g"]]).astype(f).reshape(80, 128)
    g5 = np.concatenate([inputs["ab_g_cq"][0], inputs["ab_g_ckv"][0]]).astype(f).reshape(8, 128)
    subg = inputs["diff_subln_g"][0].astype(f).reshape(2, 128)
    lvec = np.stack([inputs["diff_lq1"][0], inputs["diff_lk1"][0], inputs["diff_lq2"][0], inputs["diff_lk2"][0]], axis=1).astype(f)
    rpb = inputs["ab_rpb"][0].astype(f)
    bt = np.empty((8, NBT, 128, 128), f)
    for ti, (valid, dr, dc) in enumerate(NA_TILES):
        g = rpb[:, dr, dc]
        bt[:, ti] = np.where(valid[None], g, f(NEG))
    cosm, sinm, perm64 = _rope_tables(64)
    cosd, sind, perm128 = _rope_tables(128)
    return {
        "x": x, "ctx": ctx, "cvec": cvec, "ada_w": np.ascontiguousarray(inputs["ada_w"], dtype=f),
        "ada_b": np.ascontiguousarray(inputs["ada_b"], dtype=f), "gtab": gt, "g512": g5, "subg": subg, "lvec": lvec,
        "btab": bt, "cosm": cosm, "sinm": sinm, "perm64": perm64, "cosd": cosd, "sind": sind, "perm128": perm128,
        "ident": np.eye(128, dtype=f),
        "ab_w_in": np.ascontiguousarray(inputs["ab_w_in"][0], dtype=f), "ab_w_uq": np.ascontiguousarray(inputs["ab_w_uq"][0], dtype=f),
        "ab_w_ukv": np.ascontiguousarray(inputs["ab_w_ukv"][0], dtype=f), "ab_w_out": np.ascontiguousarray(inputs["ab_w_out"][0], dtype=f),
        "mlp_w1": np.ascontiguousarray(inputs["mlp_w1"], dtype=f), "mlp_w2": np.ascontiguousarray(inputs["mlp_w2"], dtype=f),
        "diff_w_in": np.ascontiguousarray(inputs["diff_w_in"][0], dtype=f), "diff_w_out": np.ascontiguousarray(inputs["diff_w_out"][0], dtype=f),
    }


def kernel(**inputs):
    inputs = {k: np.asarray(v) for k, v in inputs.items()}
    nc, _ = build()
    in_maps = [_host_inputs(inputs, b) for b in range(N_CORES)]
    res = run_bass_kernel_spmd(nc, in_maps, core_ids=list(range(N_CORES)))
    out = np.stack([np.asarray(res.results[b]["out"], dtype=np.float32) for b in range(N_CORES)], axis=0)
    return out
```

```python
import numpy as np
from contextlib import ExitStack
import concourse.bass as bass
import concourse.mybir as mybir
from concourse.bass_utils import run_bass_kernel_spmd

F32 = mybir.dt.float32
BF16 = mybir.dt.bfloat16
AF = mybir.ActivationFunctionType
ALU = mybir.AluOpType

COMPUTE = ("pe", "act", "dve", "pool")
ALLENG = ("pe", "act", "dve", "pool", "sp")

D = 2048
NLAT = 4096
NCTX = 256
NT = NLAT + NCTX
GROUPS = [(g * 512, 512) for g in range(8)] + [(NLAT, NCTX)]
EPS = 1e-6
NEG = -30000.0
N_CORES = 4


class MK:
    def __init__(self, nc, es):
        self.nc = nc
        self.es = es
        self.h = {"pe": nc.tensor, "act": nc.scalar, "dve": nc.vector, "pool": nc.gpsimd, "sp": nc.sync}
        self.ops = []
        self.last_w = {}
        self.readers = {}
        self.last_op = {}
        self.dcount = {}

    def _deps(self, i, eng, is_dma, reads, writes, grp=None):
        raw = set()
        oth = set()
        for b in reads:
            d = self.last_w.get(b)
            if d is not None:
                raw.add(d)
        for b in writes:
            d = self.last_w.get(b)
            if d is not None:
                oth.add(d)
            for r in self.readers.get(b, ()):
                oth.add(r)
        deps = []
        for d in sorted(raw | oth):
            od = self.ops[d]
            if od[4] is not None:
                deps.append(("d", od[4], self.dcount[od[4]]))
                continue
            if (not is_dma) and od[0] == eng and eng == "pe":
                continue
            od[3] = True
            deps.append(("c", d))
        grp = grp if grp is not None else eng
        for b in reads:
            lst = self.readers.setdefault(b, [])
            lst[:] = [r for r in lst if (self.ops[r][4] if self.ops[r][4] is not None else self.ops[r][0]) != grp]
            lst.append(i)
        for b in writes:
            self.last_w[b] = i
            self.readers[b] = []
        return deps

    def op(self, eng, fn, reads=(), writes=()):
        i = len(self.ops)
        deps = self._deps(i, eng, False, reads, writes)
        self.ops.append([eng, fn, deps, False, None])
        self.last_op[eng] = i
        return i

    def dma(self, eng, key, out, in_, reads=(), writes=()):
        i = len(self.ops)
        deps = self._deps(i, eng, True, reads, writes, grp=key)
        if self.dcount.get(key, 0) > 0:
            deps.append(("d", key, self.dcount[key]))
        self.dcount[key] = self.dcount.get(key, 0) + 16
        self.ops.append([eng, (lambda e, o=out, a=in_: e.dma_start(out=o, in_=a)), deps, False, key])
        return i

    def barrier(self):
        deps = []
        for e in COMPUTE:
            d = self.last_op.get(e)
            if d is not None:
                self.ops[d][3] = True
                deps.append(("c", d))
        for k, v in self.dcount.items():
            deps.append(("d", k, v))
        for e in ALLENG:
            self.ops.append([e, None, list(deps), False, None])
        self.last_w = {}
        self.readers = {}

    def emit(self):
        nc = self.nc
        if not hasattr(self, "esem"):
            self.esem = {e: self.es.enter_context(nc.semaphore("sem_" + e)) for e in COMPUTE}
            self.dsem = {}
            self.cnt = {e: 0 for e in COMPUTE}
            self.tok = {}
            self.waited = {}
            self.pos = 0
        esem, dsem, cnt, tok, waited = self.esem, self.dsem, self.cnt, self.tok, self.waited
        for k in self.dcount:
            if k not in dsem:
                dsem[k] = self.es.enter_context(nc.semaphore("dq_" + k))
        for i in range(self.pos, len(self.ops)):
            eng, fn, deps, signal, dkey = self.ops[i]
            h = self.h[eng]
            for dep in deps:
                if dep[0] == "c":
                    se, val = tok[dep[1]]
                    sem = esem[se]
                    kk = (eng, "c" + se)
                else:
                    sem = dsem[dep[1]]
                    val = dep[2]
                    kk = (eng, "d" + dep[1])
                if waited.get(kk, 0) >= val:
                    continue
                waited[kk] = val
                h.wait_ge(sem, val)
            if fn is None:
                continue
            ins = fn(h)
            if dkey is not None:
                ins.then_inc(dsem[dkey], 16)
            elif signal:
                cnt[eng] += 1
                ins.then_inc(esem[eng], 1)
                tok[i] = (eng, cnt[eng])
        self.pos = len(self.ops)

    def sync(self):
        self.barrier()
        self.emit()


def _rope_tables(rot_dim):
    axis_dim = rot_dim // 2
    half = axis_dim // 2
    inv_freq = (10000.0 ** (-np.arange(0, axis_dim, 2, dtype=np.float32) / np.float32(axis_dim))).astype(np.float32)
    t = np.arange(NLAT)
    row = (t // 64).astype(np.float32)
    col = (t % 64).astype(np.float32)
    ang_r = row[:, None] * inv_freq[None, :]
    ang_c = col[:, None] * inv_freq[None, :]
    cos = np.ones((rot_dim, NT), np.float32)
    sin = np.zeros((rot_dim, NT), np.float32)
    for base, ang in ((0, ang_r), (axis_dim, ang_c)):
        c = np.cos(ang).astype(np.float32).T
        s = np.sin(ang).astype(np.float32).T
        cos[base:base + half, :NLAT] = c
        cos[base + half:base + axis_dim, :NLAT] = c
        sin[base:base + half, :NLAT] = -s
        sin[base + half:base + axis_dim, :NLAT] = s
    perm = np.zeros((rot_dim, rot_dim), np.float32)
    for base in (0, axis_dim):
        for f in range(half):
            perm[base + f + half, base + f] = 1.0
            perm[base + f, base + f + half] = 1.0
    return cos, sin, perm


def _na_plan():
    def r_start(r):
        return int(np.clip(r - 4, 0, 56))
    def c_start(c):
        return np.clip(c - 8, 0, 48)
    qi = np.arange(128)
    cases = {}
    tiles = []
    plan = []
    for j in range(32):
        lst = []
        qr = 2 * j + qi // 64
        qc = qi % 64
        rs = np.array([r_start(r) for r in qr])
        cs = c_start(qc)
        for i in range(32):
            kr = 2 * i + qi // 64
            kc = qi % 64
            valid = ((kr[None, :] >= rs[:, None]) & (kr[None, :] < rs[:, None] + 8) &
                     (kc[None, :] >= cs[:, None]) & (kc[None, :] < cs[:, None] + 16))
            if not valid.any():
                continue
            key = ("i", i - j) if 2 <= j <= 29 else ("e", j, i)
            if key not in cases:
                dr = np.clip(kr[None, :] - qr[:, None] + 7, 0, 14)
                dc = np.clip(kc[None, :] - qc[:, None], -15, 15) + 15
                cases[key] = len(tiles)
                tiles.append((valid, dr, dc))
            lst.append((i, cases[key]))
        plan.append(lst)
    return plan, tiles


NA_PLAN, NA_TILES = _na_plan()
NBT = len(NA_TILES)


def build(stop_after=None, debug_out=()):
    nc = bass.Bass("TRN2", target_bir_lowering=False)
    dbg = {}

    def din(name, shape, dt=F32):
        return nc.dram_tensor(name, list(shape), dt, kind="ExternalInput").ap()

    def dscr(name, shape, dt):
        if name in debug_out:
            t = nc.dram_tensor(name, list(shape), dt, kind="ExternalOutput").ap()
            dbg[name] = t
            return t
        return nc.dram_tensor(name, list(shape), dt).ap()

    x_in = din("x", [NLAT, D])
    ctx_in = din("ctx", [NCTX, D])
    cvec = din("cvec", [2, D])
    ada_w = din("ada_w", [2, D, 6 * D])
    ada_b = din("ada_b", [2, 6 * D])
    gtab = din("gtab", [80, 128])
    g512 = din("g512", [8, 128])
    subg_in = din("subg", [2, 128])
    lvec = din("lvec", [128, 4])
    btab = din("btab", [8, NBT, 128, 128])
    cosm_in = din("cosm", [64, NT]); sinm_in = din("sinm", [64, NT]); perm64_in = din("perm64", [64, 64])
    cosd_in = din("cosd", [128, NT]); sind_in = din("sind", [128, NT]); perm128_in = din("perm128", [128, 128])
    ident_in = din("ident", [128, 128])
    w_in0 = din("ab_w_in", [D, 4160])
    w_uq = din("ab_w_uq", [512, 1536])
    w_ukv = din("ab_w_ukv", [512, 2048])
    w_out0 = din("ab_w_out", [D, D])
    mlp_w1 = din("mlp_w1", [2, D, 8192])
    mlp_w2 = din("mlp_w2", [2, 8192, D])
    w_in1 = din("diff_w_in", [D, 6144])
    w_out1 = din("diff_w_out", [D, D])
    out_d = nc.dram_tensor("out", [NLAT, D], F32, kind="ExternalOutput").ap()

    XT = dscr("XT", [16, 128, NT], F32)
    QA = dscr("QA", [8, 128, NT], BF16)
    KA = dscr("KA", [8, 128, NT], BF16)
    VA = dscr("VA", [NT, 1024], BF16)
    CQ = dscr("CQ", [4, 128, NT], F32)
    CKV = dscr("CKV", [4, 128, NT], F32)
    KR = dscr("KR", [64, NT], F32)
    OT = dscr("OT", [16, 128, NT], BF16)
    QD = dscr("QD", [16, 128, NLAT], BF16)
    KD = dscr("KD", [16, 128, NT], BF16)
    VD = dscr("VD", [NT, 2048], BF16)

    es = ExitStack()
    mk = MK(nc, es)
    lam_init = 0.8 - 0.6 * float(np.exp(-0.3))

    with es:
        uid = [0]

        def sb(name, shape, dt):
            uid[0] += 1
            return es.enter_context(nc.sbuf_tensor(f"sb{uid[0]}_{name}", list(shape), dt))

        castc = [0]

        class WS:
            def __init__(self, name, w_ap, K, blocks):
                self.kc = K // 128
                self.blocks = blocks
                self.t = []
                for bi, (c0, ncol) in enumerate(blocks):
                    t = nc.dram_tensor(f"{name}_{bi}", [128, self.kc, ncol], BF16).ap()
                    self.t.append(t)
                    src = w_ap[:, c0:c0 + ncol].rearrange("(k p) n -> p k n", p=128)
                    for k0 in range(0, self.kc, 16):
                        k1 = min(self.kc, k0 + 16)
                        sl = castc[0] % 4
                        castc[0] += 1
                        mk.dma("pool", f"cast{sl}", t[:, k0:k1, :], src[:, k0:k1, :], writes=[("castslot", sl)])
                self.name = name

        def blocks_of(n, nb):
            return [(c, min(nb, n - c)) for c in range(0, n, nb)]

        ws_in0 = WS("w_in0", w_in0, D, blocks_of(4096, 512) + [(4096, 64)])
        ws_uq = WS("w_uq", w_uq, 512, blocks_of(1536, 1536))
        ws_ukv = WS("w_ukv", w_ukv, 512, blocks_of(2048, 2048))
        ws_out0 = WS("w_out0", w_out0, D, blocks_of(D, 512))
        ws_w1 = [WS(f"w1_{l}", mlp_w1[l], D, blocks_of(8192, 512)) for l in range(2)]
        ws_w2 = [WS(f"w2_{l}", mlp_w2[l], 8192, blocks_of(D, 128)) for l in range(2)]
        ws_in1 = WS("w_in1", w_in1, D, blocks_of(6144, 512))
        ws_out1 = WS("w_out1", w_out1, D, blocks_of(D, 512))

        ident = sb("ident", [128, 128], F32)
        identb = sb("identb", [128, 128], BF16)
        onesb = sb("onesb", [128, 128], BF16)
        onesf = sb("onesf", [128, 128], F32)
        epst = sb("epst", [128, 1], F32)
        modT = sb("modT", [128, 2, 96, 2], F32)
        gT = sb("gT", [128, 80], F32)
        g5T = sb("g5T", [128, 8], F32)
        subgT = sb("subgT", [128, 2], F32)
        AS = sb("AS", [128, 2, 2, 2, 16], F32)
        lamt = sb("lamt", [128, 4], F32)
        PBt = [es.enter_context(nc.psum_tensor(f"psb{i}", [128, 512], F32)) for i in range(7)]
        PTt = es.enter_context(nc.psum_tensor("ptb", [128, 1024], BF16))

        def PB(i):
            return PBt[i], f"pb{i}"

        mk.dma("sp", "k_id", ident[:], ident_in, writes=["ident"])
        mk.op("dve", lambda e: e.tensor_copy(out=identb[:], in_=ident[:]), reads=["ident"], writes=["identb"])
        mk.op("dve", lambda e: e.memset(onesb[:], 1.0), writes=["onesb"])
        mk.op("dve", lambda e: e.memset(onesf[:], 1.0), writes=["onesf"])
        mk.op("dve", lambda e: e.memset(epst[:], EPS), writes=["epst"])

        with ExitStack() as ph:
            def sbp(name, shape, dt):
                uid[0] += 1
                return ph.enter_context(nc.sbuf_tensor(f"sb{uid[0]}_{name}", list(shape), dt))
            gl = sbp("gl", [80, 128], F32)
            g5l = sbp("g5l", [8, 128], F32)
            sgl = sbp("sgl", [2, 128], F32)
            cl = sbp("cl", [2, D], F32)
            sc = sbp("sc", [2, D], F32)
            scT = sbp("scT", [128, 16, 2], F32)
            awt = [sbp(f"awt{i}", [128, 16, 512], F32) for i in range(2)]
            modtm = sbp("modtm", [2, 6 * D], F32)
            abt = sbp("abt", [2, 6 * D], F32)
            mk.dma("sp", "k0", gl[:], gtab, writes=["gl"])
            mk.dma("sp", "k1", g5l[:], g512, writes=["g5l"])
            mk.dma("sp", "k2", sgl[:], subg_in, writes=["sgl"])
            mk.dma("sp", "k3", cl[:], cvec, writes=["cl"])
            mk.dma("sp", "k4", lamt[:], lvec, writes=["lamt"])
            p0, p0n = PB(0)
            mk.op("pe", lambda e: e.transpose(out=p0[:, 0:80], in_=gl[:], identity=ident[0:80, 0:80]),
                  reads=["gl", "ident"], writes=[p0n])
            mk.op("pe", lambda e: e.transpose(out=p0[:, 80:88], in_=g5l[:], identity=ident[0:8, 0:8]),
                  reads=["g5l", "ident"], writes=[p0n])
            mk.op("pe", lambda e: e.transpose(out=p0[:, 88:90], in_=sgl[:], identity=ident[0:2, 0:2]),
                  reads=["sgl", "ident"], writes=[p0n])
            mk.op("dve", lambda e: e.tensor_copy(out=gT[:], in_=p0[:, 0:80]), reads=[p0n], writes=["gT"])
            mk.op("dve", lambda e: e.tensor_copy(out=g5T[:], in_=p0[:, 80:88]), reads=[p0n], writes=["g5T"])
            mk.op("dve", lambda e: e.tensor_copy(out=subgT[:], in_=p0[:, 88:90]), reads=[p0n], writes=["subgT"])
            lw = sbp("lw", [128, 4], F32)
            mk.op("dve", lambda e: e.tensor_tensor(out=lw[:, 0:1], in0=lamt[:, 0:1], in1=lamt[:, 1:2], op=ALU.mult),
                  reads=["lamt"], writes=["lw0"])
            mk.op("dve", lambda e: e.tensor_tensor(out=lw[:, 1:2], in0=lamt[:, 2:3], in1=lamt[:, 3:4], op=ALU.mult),
                  reads=["lamt"], writes=["lw1"])
            p1, p1n = PB(1)
            mk.op("pe", lambda e: e.matmul(p1[:, 0:2], lhsT=onesf[:], rhs=lw[:, 0:2], start=True, stop=True),
                  reads=["lw0", "lw1", "onesf"], writes=[p1n])
            mk.op("act", lambda e: e.activation(out=lw[:, 2:4], in_=p1[:, 0:2], func=AF.Exp), reads=[p1n], writes=["lw2"])
            mk.op("dve", lambda e: e.tensor_tensor(out=lamt[:, 0:1], in0=lw[:, 3:4], in1=lw[:, 2:3], op=ALU.subtract),
                  reads=["lw2", "lamt"], writes=["lamt0"])
            mk.op("dve", lambda e: e.tensor_scalar(out=lamt[:, 1:2], in0=lamt[:, 0:1], scalar1=-lam_init, scalar2=None,
                                                   op0=ALU.add), reads=["lamt0"], writes=["neglam"])
            mk.op("act", lambda e: e.activation(out=sc[:], in_=cl[:], func=AF.Silu), reads=["cl"], writes=["sc"])
            p2, p2n = PB(2)
            for j in range(16):
                mk.op("pe", lambda e, j=j: e.transpose(out=p2[:, 2 * j:2 * j + 2], in_=sc[:, j * 128:(j + 1) * 128],
                                                       identity=ident[0:2, 0:2]), reads=["sc", "ident"], writes=[p2n])
            mk.op("dve", lambda e: e.tensor_copy(out=scT[:].rearrange("p j m -> p (j m)"), in_=p2[:, 0:32]),
                  reads=[p2n], writes=["scT"])
            for l in range(2):
                mk.dma("sp", "k0", abt[0:1, :], ada_b[l:l + 1, :], writes=["abt"])
                mk.dma("sp", "k1", abt[1:2, :], ada_b[l:l + 1, :], writes=["abt"])
                for nb in range(24):
                    wi = (l * 24 + nb) % 2
                    mk.dma("sp", f"aw{wi}", awt[wi][:],
                           ada_w[l, :, nb * 512:(nb + 1) * 512].rearrange("(k p) n -> p k n", p=128),
                           writes=[f"awt{wi}"])
                    pp, ppn = PB(3 + nb % 2)
                    for k in range(16):
                        mk.op("pe", lambda e, k=k, pp=pp, wi=wi: e.matmul(pp[0:2, :], lhsT=scT[:, k, :], rhs=awt[wi][:, k, :],
                                                                           start=(k == 0), stop=(k == 15)),
                              reads=["scT", f"awt{wi}"], writes=[ppn])
                    mk.op("dve", lambda e, pp=pp, nb=nb: e.tensor_tensor(out=modtm[:, nb * 512:(nb + 1) * 512], in0=pp[0:2, :],
                                                                         in1=abt[:, nb * 512:(nb + 1) * 512], op=ALU.add),
                          reads=[ppn, "abt"], writes=["modtm"])
                p5, p5n = PB(5)
                for c in range(96):
                    mk.op("pe", lambda e, c=c: e.transpose(out=p5[:, 2 * c:2 * c + 2], in_=modtm[:, c * 128:(c + 1) * 128],
                                                           identity=ident[0:2, 0:2]), reads=["modtm", "ident"], writes=[p5n])
                mk.op("dve", lambda e, l=l: e.tensor_copy(out=modT[:, l].rearrange("p c m -> p (c m)"), in_=p5[:, 0:192]),
                      reads=[p5n], writes=["modT"])
                for which in range(2):
                    for m in range(2):
                        ci = 1 + 3 * which
                        gv = which * 2 + l
                        mk.op("dve", lambda e, l=l, which=which, m=m, ci=ci, gv=gv: e.scalar_tensor_tensor(
                            out=AS[:, l, which, m, :], in0=modT[:, l, ci * 16:(ci + 1) * 16, m], scalar=1.0,
                            in1=gT[:, gv * 16:(gv + 1) * 16], op0=ALU.add, op1=ALU.mult),
                            reads=["modT", "gT"], writes=["AS"])
            mk.sync()

        def modv(l, ci, m, j):
            return modT[:, l, ci * 16 + j, m:m + 1]

        with ExitStack() as ph:
            def sbp(name, shape, dt):
                uid[0] += 1
                return ph.enter_context(nc.sbuf_tensor(f"sb{uid[0]}_{name}", list(shape), dt))
            xin = [sbp(f"xin{i}", [128, D], F32) for i in range(2)]
            xo = [sbp(f"xo{i}", [128, 16, 128], F32) for i in range(2)]
            for t in range(NT // 128):
                i = t % 2
                src = x_in[t * 128:(t + 1) * 128, :] if t < 32 else ctx_in[(t - 32) * 128:(t - 31) * 128, :]
                mk.dma("sp", f"xin{i}", xin[i][:], src, writes=[f"xin{i}"])
                for q in range(4):
                    pp, ppn = PB(q)
                    for jj in range(4):
                        j = q * 4 + jj
                        mk.op("pe", lambda e, i=i, j=j, jj=jj, pp=pp: e.transpose(
                            out=pp[:, jj * 128:(jj + 1) * 128], in_=xin[i][:, j * 128:(j + 1) * 128], identity=ident[:]),
                            reads=[f"xin{i}", "ident"], writes=[ppn])
                    eng = "dve" if q % 2 == 0 else "act"
                    if eng == "dve":
                        mk.op("dve", lambda e, i=i, q=q, pp=pp: e.tensor_copy(
                            out=xo[i][:, q * 4:(q + 1) * 4, :].rearrange("p a b -> p (a b)"), in_=pp[:, :]),
                            reads=[ppn], writes=[f"xo{i}"])
                    else:
                        mk.op("act", lambda e, i=i, q=q, pp=pp: e.copy(
                            out=xo[i][:, q * 4:(q + 1) * 4, :].rearrange("p a b -> p (a b)"), in_=pp[:, :]),
                            reads=[ppn], writes=[f"xo{i}"])
                mk.dma("pool", f"xo{i}", XT[:, :, t * 128:(t + 1) * 128].rearrange("j p t -> p j t"), xo[i][:],
                       reads=[f"xo{i}"], writes=[("XT", t // 4)])
            mk.sync()
        if stop_after == 2:
            mk.emit()
            return nc, dbg

        def norm_fm(src, srcn, kcn, T, nfeat, sq, sqn, rs, rsn, Afn, Sfn, dstfn, dstn, tmp=None, ssbank=6):
            mk.op("act", lambda e: e.activation(out=sq[:, 0:kcn, 0:T], in_=src[:, 0:kcn, 0:T], func=AF.Square),
                  reads=[srcn], writes=[sqn])
            ps, psn = PB(ssbank)
            for j in range(kcn):
                mk.op("pe", lambda e, j=j: e.matmul(ps[:, 0:T], lhsT=onesb[:], rhs=sq[:, j, 0:T], start=(j == 0),
                                                    stop=(j == kcn - 1)), reads=[sqn, "onesb"], writes=[psn])
            mk.op("act", lambda e: e.activation(out=rs[:, 0:T], in_=ps[:, 0:T], func=AF.Sqrt, bias=epst[:, 0:1],
                                                scale=1.0 / nfeat), reads=[psn, "epst"], writes=[rsn])
            mk.op("dve", lambda e: e.reciprocal(out=rs[:, 0:T], in_=rs[:, 0:T]), reads=[rsn], writes=[rsn])
            for j in range(kcn):
                if Sfn is None:
                    mk.op("dve", lambda e, j=j: e.scalar_tensor_tensor(out=dstfn(j), in0=src[:, j, 0:T], scalar=Afn(j),
                                                                       in1=rs[:, 0:T], op0=ALU.mult, op1=ALU.mult),
                          reads=[srcn, rsn], writes=[dstn])
                else:
                    tb, tbn = tmp[j % 2]
                    mk.op("dve", lambda e, j=j, tb=tb: e.scalar_tensor_tensor(out=tb[:, 0:T], in0=src[:, j, 0:T], scalar=Afn(j),
                                                                              in1=rs[:, 0:T], op0=ALU.mult, op1=ALU.mult),
                          reads=[srcn, rsn], writes=[tbn])
                    mk.op("act", lambda e, j=j, tb=tb: e.activation(out=dstfn(j), in_=tb[:, 0:T], func=AF.Identity,
                                                                    bias=Sfn(j), scale=1.0), reads=[tbn], writes=[dstn])

        wcount = [0]

        def load_w(wbufs, ws, bi):
            i = wcount[0] % len(wbufs)
            wcount[0] += 1
            kc = ws.kc
            ncol = ws.blocks[bi][1]
            view = wbufs[i][:, 0:kc * ncol].rearrange("p (k n) -> p k n", k=kc)
            mk.dma("sp", f"w{i}", view, ws.t[bi], reads=[(ws.name, bi)], writes=[f"wbuf{i}"])
            return view, f"wbuf{i}"

        ppc = [0]

        def proj_fm(wv, wn, act, actn, kcn, ncols, T, epi, banks=(0, 1, 2, 3, 4, 5)):
            for c in range((ncols + 127) // 128):
                M = min(128, ncols - c * 128)
                b = banks[ppc[0] % len(banks)]
                ppc[0] += 1
                ps, psn = PB(b)
                for k in range(kcn):
                    mk.op("pe", lambda e, k=k, c=c, M=M, ps=ps: e.matmul(ps[0:M, 0:T], lhsT=wv[:, k, c * 128:c * 128 + M],
                                                                         rhs=act[:, k, 0:T], start=(k == 0), stop=(k == kcn - 1)),
                          reads=[wn, actn], writes=[psn])
                epi(c, ps, psn, M)

        def proj_tm(wv, wn, act, actn, kcn, ncols, T, epi, banks=(0, 1, 2, 3, 4, 5)):
            for t in range(T // 128):
                b = banks[ppc[0] % len(banks)]
                ppc[0] += 1
                ps, psn = PB(b)
                for k in range(kcn):
                    mk.op("pe", lambda e, k=k, t=t, ps=ps: e.matmul(ps[:, 0:ncols], lhsT=act[:, k, t * 128:(t + 1) * 128],
                                                                    rhs=wv[:, k, 0:ncols], start=(k == 0), stop=(k == kcn - 1)),
                          reads=[wn, actn], writes=[psn])
                epi(t, ps, psn)

        stc = [0]

        def rope_fm(ps, psn, M, T, t0, cosT, sinT, permT, dst, dstn, tmps):
            i = stc[0] % 2
            stc[0] += 1
            xs, xsn = tmps[0][i]
            t1, t1n = tmps[1][i]
            t2, t2n = tmps[2][i]
            mk.op("act", lambda e: e.copy(out=xs[0:M, 0:T], in_=ps[0:M, 0:T]), reads=[psn], writes=[xsn])
            pq, pqn = PB(4 + i)
            mk.op("pe", lambda e: e.matmul(pq[0:M, 0:T], lhsT=permT[0:M, 0:M], rhs=xs[0:M, 0:T], start=True, stop=True),
                  reads=[xsn, "perm"], writes=[pqn])
            mk.op("pool", lambda e: e.tensor_tensor(out=t1[0:M, 0:T], in0=xs[0:M, 0:T], in1=cosT[0:M, t0:t0 + T], op=ALU.mult),
                  reads=[xsn, "ropetab"], writes=[t1n])
            mk.op("dve", lambda e: e.tensor_tensor(out=t2[0:M, 0:T], in0=pq[0:M, 0:T], in1=sinT[0:M, t0:t0 + T], op=ALU.mult),
                  reads=[pqn, "ropetab"], writes=[t2n])
            mk.op("pool", lambda e: e.tensor_tensor(out=dst, in0=t1[0:M, 0:T], in1=t2[0:M, 0:T], op=ALU.add),
                  reads=[t1n, t2n], writes=[dstn])

        def store(dst, src, srcn, dstn):
            mk.dma("pool", srcn if isinstance(srcn, str) else "st", dst, src, reads=[srcn], writes=[dstn])

        with ExitStack() as ph:
            def sbp(name, shape, dt):
                uid[0] += 1
                return ph.enter_context(nc.sbuf_tensor(f"sb{uid[0]}_{name}", list(shape), dt))
            xg = sbp("xg", [128, 16, 512], F32)
            sq = sbp("sq", [128, 16, 512], BF16)
            hT = sbp("hT", [128, 16, 512], BF16)
            rs = sbp("rs", [128, 512], F32)
            tmpa = [(sbp(f"tmpa{i}", [128, 512], F32), f"tmpa{i}") for i in range(2)]
            wbufs = [sbp(f"wbuf{i}", [128, 8192], BF16) for i in range(3)]
            stb = [sbp(f"stb{i}", [128, 512], BF16) for i in range(4)]
            stf = [sbp(f"stf{i}", [128, 512], F32) for i in range(4)]
            cnt = [0]
            for gi, (t0, T) in enumerate(GROUPS):
                m = 0 if t0 < NLAT else 1
                mk.dma("sp", "xg", xg[:, :, 0:T], XT[:, :, t0:t0 + T].rearrange("j p t -> p j t"), writes=["xg"])
                norm_fm(xg, "xg", 16, T, D, sq, "sq", rs, "rs",
                        lambda j, m=m: AS[:, 0, 0, m, j:j + 1], lambda j, m=m: modv(0, 0, m, j),
                        lambda j, T=T: hT[:, j, 0:T], "hT", tmp=tmpa)
                for bi, (c0, ncol) in enumerate(ws_in0.blocks):
                    wv, wn = load_w(wbufs, ws_in0, bi)
                    if 2560 <= c0 < 3584:
                        def epi_v(t, ps, psn, c0=c0, t0=t0):
                            i = cnt[0] % 4; cnt[0] += 1
                            mk.op("act", lambda e: e.copy(out=stb[i][:, :], in_=ps[:, :]), reads=[psn], writes=[f"stb{i}"])
                            store(VA[t0 + t * 128:t0 + (t + 1) * 128, c0 - 2560:c0 - 2560 + 512], stb[i][:, :], f"stb{i}", ("VA", t0))
                        proj_tm(wv, wn, hT, "hT", 16, ncol, T, epi_v)
                        continue

                    def epi(c, ps, psn, M, c0=c0, t0=t0, T=T):
                        col = c0 + c * 128
                        i = cnt[0] % 4; cnt[0] += 1
                        if col < 1024:
                            mk.op("act", lambda e: e.mul(out=stb[i][:, 0:T], in_=ps[:, 0:T], mul=128 ** -0.5), reads=[psn], writes=[f"stb{i}"])
                            store(QA[col // 128, :, t0:t0 + T], stb[i][:, 0:T], f"stb{i}", ("QA", t0))
                        elif col < 1536:
                            mk.op("dve", lambda e: e.tensor_copy(out=stf[i][:, 0:T], in_=ps[:, 0:T]), reads=[psn], writes=[f"stf{i}"])
                            store(CQ[(col - 1024) // 128, :, t0:t0 + T], stf[i][:, 0:T], f"stf{i}", ("CQ", t0))
                        elif col < 2560:
                            mk.op("dve", lambda e: e.tensor_copy(out=stb[i][:, 0:T], in_=ps[:, 0:T]), reads=[psn], writes=[f"stb{i}"])
                            store(KA[(col - 1536) // 128, :, t0:t0 + T], stb[i][:, 0:T], f"stb{i}", ("KA", t0))
                        elif col < 4096:
                            mk.op("dve", lambda e: e.tensor_copy(out=stf[i][:, 0:T], in_=ps[:, 0:T]), reads=[psn], writes=[f"stf{i}"])
                            store(CKV[(col - 3584) // 128, :, t0:t0 + T], stf[i][:, 0:T], f"stf{i}", ("CKV", t0))
                        else:
                            mk.op("dve", lambda e: e.tensor_copy(out=stf[i][0:64, 0:T], in_=ps[0:64, 0:T]), reads=[psn], writes=[f"stf{i}"])
                            store(KR[:, t0:t0 + T], stf[i][0:64, 0:T], f"stf{i}", ("KR", t0))
                    proj_fm(wv, wn, hT, "hT", 16, ncol, T, epi)
            mk.sync()
        if stop_after == 3:
            mk.emit()
            return nc, dbg

        SB = (0, 1, 6)

        def attn_chunk(qparts, kparts, rnames, vaug, vn, dv, ktiles, nq, scale, pTb, bias=None, acc0=2):
            nu = nq // 128
            accs = [PB(acc0 + u) for u in range(nu)]
            nk = len(ktiles)
            np_ = len(qparts)

            def S(ti):
                t = ktiles[ti]
                ps, psn = PB(SB[ti % 3])
                hasb = bias is not None and bias(ti) is not None
                for pi in range(np_):
                    last = (pi == np_ - 1) and not hasb
                    mk.op("pe", lambda e, pi=pi, t=t, ps=ps, last=last: e.matmul(ps[:, 0:nq], lhsT=kparts[pi](t), rhs=qparts[pi],
                                                                                 start=(pi == 0), stop=last),
                          reads=rnames, writes=[psn])
                if hasb:
                    mk.op("pe", lambda e, ti=ti, ps=ps: e.matmul(ps[:, 0:nq], lhsT=bias(ti), rhs=identb[:, 0:nq], start=False, stop=True),
                          reads=["bias", "identb"], writes=[psn])
                pT, pTn = pTb[ti % 3]
                mk.op("act", lambda e, ps=ps, pT=pT: e.activation(out=pT[:, 0:nq], in_=ps[:, 0:nq], func=AF.Exp, scale=scale),
                      reads=[psn], writes=[pTn])

            def PV(ti):
                t = ktiles[ti]
                pT, pTn = pTb[ti % 3]
                for u in range(nu):
                    acc, accn = accs[u]
                    mk.op("pe", lambda e, u=u, t=t, acc=acc, pT=pT, ti=ti: e.matmul(acc[:, 0:dv + 1], lhsT=pT[:, u * 128:(u + 1) * 128],
                                                                                    rhs=vaug(t), start=(ti == 0), stop=(ti == nk - 1)),
                          reads=[pTn, vn], writes=[accn])

            for ti in range(min(2, nk)):
                S(ti)
            for ti in range(nk):
                if ti + 2 < nk:
                    S(ti + 2)
                PV(ti)
            return accs

        def finish_simple(accs, nq, dv, rec, ob, obn, oT, oTn, dst_chunk_ap, dstn):
            nu = nq // 128
            for u, (acc, accn) in enumerate(accs):
                mk.op("dve", lambda e, u=u, acc=acc: e.reciprocal(out=rec[:, u:u + 1], in_=acc[:, dv:dv + 1]), reads=[accn], writes=[("rec", u)])
                mk.op("dve", lambda e, u=u, acc=acc: e.tensor_scalar(out=ob[:, u, :], in0=acc[:, 0:dv], scalar1=rec[:, u:u + 1], scalar2=None,
                                                                     op0=ALU.mult), reads=[accn, ("rec", u)], writes=[(obn, u)])
                mk.op("pe", lambda e, u=u: e.transpose(out=PTt[:, u * 128:(u + 1) * 128], in_=ob[:, u, :], identity=identb[:]),
                      reads=[(obn, u), "identb"], writes=["ptb"])
            mk.op("act", lambda e: e.copy(out=oT[:, 0:nq], in_=PTt[:, 0:nq]), reads=["ptb"], writes=[oTn])
            store(dst_chunk_ap, oT[:, 0:nq], oTn, dstn)

        with ExitStack() as ph:
            def sbp(name, shape, dt):
                uid[0] += 1
                return ph.enter_context(nc.sbuf_tensor(f"sb{uid[0]}_{name}", list(shape), dt))
            ckvn = sbp("ckvn", [128, 4, NT], BF16)
            cqn = sbp("cqn", [128, 4, NT], BF16)
            krT = sbp("krT", [64, NT], BF16)
            cosm = sbp("cosm", [64, NT], F32)
            sinm = sbp("sinm", [64, NT], F32)
            perm64 = sbp("perm64", [64, 64], F32)
            cg = sbp("cg", [128, 4, 512], F32)
            sq = sbp("sq4", [128, 4, 512], BF16)
            rs = sbp("rs4", [128, 512], F32)
            tm = [[(sbp(f"rt{a}{i}", [128, 512], F32), f"rt{a}{i}") for i in range(2)] for a in range(3)]
            wuqs = [sbp(f"wuq{i}", [128, 4, 192], BF16) for i in range(2)]
            wukvs = [sbp(f"wukv{i}", [128, 4, 256], BF16) for i in range(2)]
            kT = sbp("kT", [128, NT], BF16)
            qT = sbp("qT", [128, NT], BF16)
            qrT = sbp("qrT", [64, NT], BF16)
            vaug = sbp("vaug", [128, 34, 129], BF16)
            pTb = [(sbp(f"pT{i}", [128, 512], BF16), f"pT{i}") for i in range(3)]
            rec = sbp("rec", [128, 8], F32)
            ob = sbp("ob", [128, 4, 128], BF16)
            oT = [sbp(f"oT{i}", [128, 512], BF16) for i in range(2)]
            bt = sbp("bt", [128, NBT, 128], BF16)
            mk.dma("sp", "k0", cosm[:], cosm_in, writes=["ropetab"])
            mk.dma("sp", "k1", sinm[:], sinm_in, writes=["ropetab"])
            mk.dma("sp", "k2", perm64[:], perm64_in, writes=["perm"])
            mk.op("dve", lambda e: e.memset(vaug[:, :, 128:129], 1.0), writes=["vaug"])
            for gi, (t0, T) in enumerate(GROUPS):
                for (src_d, dstt, dn, goff) in ((CKV, ckvn, "ckvn", 4), (CQ, cqn, "cqn", 0)):
                    mk.dma("sp", "cg", cg[:, :, 0:T], src_d[:, :, t0:t0 + T].rearrange("j p t -> p j t"), writes=["cg"])
                    norm_fm(cg, "cg", 4, T, 512, sq, "sq4", rs, "rs4", lambda j, goff=goff: g5T[:, goff + j:goff + j + 1], None,
                            lambda j, dstt=dstt, t0=t0, T=T: dstt[:, j, t0:t0 + T], dn)
                mk.dma("sp", "cg", cg[0:64, 0, 0:T], KR[:, t0:t0 + T], writes=["cg"])
                pq, pqn = PB(4 + gi % 2)
                mk.op("pe", lambda e, pq=pq, T=T: e.matmul(pq[0:64, 0:T], lhsT=perm64[:, :], rhs=cg[0:64, 0, 0:T], start=True, stop=True),
                      reads=["cg", "perm"], writes=[pqn])
                t1, t1n = tm[1][gi % 2]
                t2, t2n = tm[2][gi % 2]
                mk.op("pool", lambda e, t1=t1, t0=t0, T=T: e.tensor_tensor(out=t1[0:64, 0:T], in0=cg[0:64, 0, 0:T], in1=cosm[:, t0:t0 + T], op=ALU.mult),
                      reads=["cg", "ropetab"], writes=[t1n])
                mk.op("dve", lambda e, t2=t2, pq=pq, t0=t0, T=T: e.tensor_tensor(out=t2[0:64, 0:T], in0=pq[0:64, 0:T], in1=sinm[:, t0:t0 + T], op=ALU.mult),
                      reads=[pqn, "ropetab"], writes=[t2n])
                mk.op("pool", lambda e, t1=t1, t2=t2, t0=t0, T=T: e.tensor_tensor(out=krT[:, t0:t0 + T], in0=t1[0:64, 0:T], in1=t2[0:64, 0:T], op=ALU.add),
                      reads=[t1n, t2n], writes=["krT"])
            mla_scale = 192 ** -0.5
            for h in range(8):
                wuq = wuqs[h % 2]; wukv = wukvs[h % 2]
                wuqn = f"wuq{h % 2}"; wukvn = f"wukv{h % 2}"
                mk.dma("sp", wuqn, wuq[:], ws_uq.t[0][:, :, h * 192:(h + 1) * 192], reads=[("w_uq", 0)], writes=[wuqn])
                mk.dma("sp", wukvn, wukv[:], ws_ukv.t[0][:, :, h * 256:(h + 1) * 256], reads=[("w_ukv", 0)], writes=[wukvn])
                for gi, (t0, T) in enumerate(GROUPS):
                    def epi_k(c, ps, psn, M, t0=t0, T=T):
                        mk.op("dve", lambda e: e.tensor_copy(out=kT[:, t0:t0 + T], in_=ps[:, 0:T]), reads=[psn], writes=["kT"])
                    proj_fm(wukv[:, :, 0:128], wukvn, ckvn[:, :, t0:t0 + T], "ckvn", 4, 128, T, epi_k)
                    def epi_q(c, ps, psn, M, t0=t0, T=T):
                        mk.op("act", lambda e: e.copy(out=qT[:, t0:t0 + T], in_=ps[:, 0:T]), reads=[psn], writes=["qT"])
                    proj_fm(wuq[:, :, 0:128], wuqn, cqn[:, :, t0:t0 + T], "cqn", 4, 128, T, epi_q)
                    def epi_qr(c, ps, psn, M, t0=t0, T=T):
                        rope_fm(ps, psn, 64, T, t0, cosm, sinm, perm64, qrT[:, t0:t0 + T], "qrT", tm)
                    proj_fm(wuq[:, :, 128:192], wuqn, cqn[:, :, t0:t0 + T], "cqn", 4, 64, T, epi_qr)
                    def epi_vv(t, ps, psn, t0=t0):
                        mk.op("dve", lambda e: e.tensor_copy(out=vaug[:, t0 // 128 + t, 0:128], in_=ps[:, 0:128]), reads=[psn], writes=["vaug"])
                    proj_tm(wukv[:, :, 128:256], wukvn, ckvn[:, :, t0:t0 + T], "ckvn", 4, 128, T, epi_vv)
                for gi, (t0, T) in enumerate(GROUPS):
                    ktiles = list(range(34)) if t0 < NLAT else [32, 33]
                    accs = attn_chunk([qT[:, t0:t0 + T], qrT[:, t0:t0 + T]],
                                      [lambda t: kT[:, t * 128:(t + 1) * 128], lambda t: krT[:, t * 128:(t + 1) * 128]],
                                      ["qT", "qrT", "kT", "krT"], lambda t: vaug[:, t, :], "vaug", 128, ktiles, T, mla_scale, pTb)
                    finish_simple(accs, T, 128, rec, ob, "ob", oT[gi % 2], f"oT{gi % 2}", OT[8 + h, :, t0:t0 + T], ("OT", 8 + h))
            for h in range(8):
                mk.dma("sp", "k0", qT[:, :], QA[h], writes=["qT"])
                mk.dma("sp", "k1", kT[:, :], KA[h], writes=["kT"])
                mk.dma("sp", "k2", vaug[:, :, 0:128], VA[:, h * 128:(h + 1) * 128].rearrange("(t p) c -> p t c", p=128), writes=["vaug"])
                mk.dma("pool", "btl", bt[:], btab[h].rearrange("t q k -> q t k"), writes=["bias"])
                for g in range(9):
                    nq = 512 if g < 8 else 256
                    nu = nq // 128
                    groups = []
                    for u in range(nu):
                        jq = g * 4 + u
                        if jq < 32:
                            lst = NA_PLAN[jq] + [(32, None), (33, None)]
                        else:
                            lst = [(32, None), (33, None)]
                        for s0 in range(0, len(lst), 4):
                            groups.append((u, jq, lst[s0:s0 + 4], s0 == 0, s0 + 4 >= len(lst)))
                    accs_all = [PB(2 + u) for u in range(nu)]

                    def S_na(n):
                        u, jq, items, first, last = groups[n]
                        ps, psn = PB(SB[n % 3])
                        for i, (t, bi) in enumerate(items):
                            mk.op("pe", lambda e, i=i, t=t, jq=jq, ps=ps, bi=bi: e.matmul(
                                ps[:, i * 128:(i + 1) * 128], lhsT=kT[:, t * 128:(t + 1) * 128], rhs=qT[:, jq * 128:(jq + 1) * 128],
                                start=True, stop=(bi is None)), reads=["qT", "kT"], writes=[psn])
                            if bi is not None:
                                mk.op("pe", lambda e, i=i, ps=ps, bi=bi: e.matmul(ps[:, i * 128:(i + 1) * 128], lhsT=bt[:, bi, :], rhs=identb[:, :],
                                                                                 start=False, stop=True), reads=["bias", "identb"], writes=[psn])
                        pT, pTn = pTb[n % 3]
                        w = len(items) * 128
                        mk.op("act", lambda e, ps=ps, pT=pT, w=w: e.activation(out=pT[:, 0:w], in_=ps[:, 0:w], func=AF.Exp, scale=1.0),
                              reads=[psn], writes=[pTn])

                    def PV_na(n):
                        u, jq, items, first, last = groups[n]
                        pT, pTn = pTb[n % 3]
                        acc, accn = accs_all[u]
                        for i, (t, bi) in enumerate(items):
                            mk.op("pe", lambda e, i=i, t=t, acc=acc, pT=pT, st=(first and i == 0), sp_=(last and i == len(items) - 1): e.matmul(
                                acc[:, 0:129], lhsT=pT[:, i * 128:(i + 1) * 128], rhs=vaug[:, t, :], start=st, stop=sp_),
                                reads=[pTn, "vaug"], writes=[accn])

                    for n in range(min(2, len(groups))):
                        S_na(n)
                    for n in range(len(groups)):
                        if n + 2 < len(groups):
                            S_na(n + 2)
                        PV_na(n)
                    finish_simple(accs_all, nq, 128, rec, ob, "ob", oT[g % 2], f"oT{g % 2}", OT[h, :, g * 512:g * 512 + nq], ("OT", h))
            mk.sync()
        if stop_after == 5:
            mk.emit()
            return nc, dbg

        def post_phase(l, ws_out, final):
            with ExitStack() as ph:
                def sbp(name, shape, dt):
                    uid[0] += 1
                    return ph.enter_context(nc.sbuf_tensor(f"sb{uid[0]}_{name}", list(shape), dt))
                og = sbp("og", [128, 16, 512], BF16)
                xg = sbp("xg", [128, 16, 512], F32)
                sq = sbp("sq", [128, 16, 512], BF16)
                hT = sbp("hT", [128, 16, 512], BF16)
                mid = sbp("mid", [128, 32, 512], BF16)
                rs = sbp("rs", [128, 512], F32)
                tmpa = [(sbp(f"tmpa{i}", [128, 512], F32), f"tmpa{i}") for i in range(2)]
                wbufs = [sbp(f"wbuf{i}", [128, 8192], BF16) for i in range(3)]
                yo = [sbp(f"yo{i}", [128, 4, 128], F32) for i in range(2)]
                groups = GROUPS[:8] if final else GROUPS
                for gi, (t0, T) in enumerate(groups):
                    m = 0 if t0 < NLAT else 1
                    nsrc = NLAT if l == 1 else NT
                    mk.dma("sp", "og", og[:, :, 0:T], OT[:, :, t0:t0 + T].rearrange("j p t -> p j t"), writes=["og"])
                    mk.dma("sp", "xg", xg[:, :, 0:T], XT[:, :, t0:t0 + T].rearrange("j p t -> p j t"), writes=["xg"])
                    for bi, (c0, ncol) in enumerate(ws_out.blocks):
                        wv, wn = load_w(wbufs, ws_out, bi)
                        def epi(c, ps, psn, M, c0=c0, T=T, m=m):
                            j = c0 // 128 + c
                            mk.op("dve", lambda e: e.scalar_tensor_tensor(out=xg[:, j, 0:T], in0=ps[:, 0:T], scalar=modv(l, 2, m, j),
                                                                          in1=xg[:, j, 0:T], op0=ALU.mult, op1=ALU.add),
                                  reads=[psn, "xg"], writes=["xg"])
                        proj_fm(wv, wn, og, "og", 16, ncol, T, epi)
                    norm_fm(xg, "xg", 16, T, D, sq, "sq", rs, "rs",
                            lambda j, m=m: AS[:, l, 1, m, j:j + 1], lambda j, m=m: modv(l, 3, m, j),
                            lambda j, T=T: hT[:, j, 0:T], "hT", tmp=tmpa)
                    for half in range(2):
                        for hb in range(8):
                            wv, wn = load_w(wbufs, ws_w1[l], half * 8 + hb)
                            def epi1(c, ps, psn, M, hb=hb, T=T):
                                kk = hb * 4 + c
                                tb, tbn = tmpa[kk % 2]
                                mk.op("act", lambda e: e.activation(out=tb[:, 0:T], in_=ps[:, 0:T], func=AF.Relu), reads=[psn], writes=[tbn])
                                mk.op("pool", lambda e: e.tensor_tensor(out=mid[:, kk, 0:T], in0=tb[:, 0:T], in1=tb[:, 0:T], op=ALU.mult),
                                      reads=[tbn], writes=["mid"])
                            proj_fm(wv, wn, hT, "hT", 16, 512, T, epi1)
                        for nb in range(16):
                            i = wcount[0] % len(wbufs)
                            wcount[0] += 1
                            view = wbufs[i][:, 0:32 * 128].rearrange("p (k n) -> p k n", k=32)
                            mk.dma("sp", f"w{i}", view, ws_w2[l].t[nb][:, half * 32:(half + 1) * 32, :], reads=[(ws_w2[l].name, nb)],
                                   writes=[f"wbuf{i}"])
                            def epi2(c, ps, psn, M, nb=nb, T=T, m=m):
                                mk.op("dve", lambda e: e.scalar_tensor_tensor(out=xg[:, nb, 0:T], in0=ps[:, 0:T], scalar=modv(l, 5, m, nb),
                                                                              in1=xg[:, nb, 0:T], op0=ALU.mult, op1=ALU.add),
                                      reads=[psn, "xg"], writes=["xg"])
                            proj_fm(view, f"wbuf{i}", mid, "mid", 32, 128, T, epi2)
                    if not final:
                        mk.dma("pool", "xgs", XT[:, :, t0:t0 + T].rearrange("j p t -> p j t"), xg[:, :, 0:T], reads=["xg"], writes=[("XT", gi)])
                    else:
                        xn = sbp(f"xn{gi}", [1, 1], F32) if False else None
                        norm_fm(xg, "xg", 16, T, D, sq, "sq", rs, "rs", lambda j: gT[:, 64 + j:65 + j], None,
                                lambda j, T=T: xg[:, j, 0:T], "xg")
                        for tt in range(T // 128):
                            for q in range(4):
                                yi = (tt * 4 + q) % 2
                                pp, ppn = PB(q)
                                for jj in range(4):
                                    j = q * 4 + jj
                                    mk.op("pe", lambda e, j=j, jj=jj, tt=tt, pp=pp: e.transpose(
                                        out=pp[:, jj * 128:(jj + 1) * 128], in_=xg[:, j, tt * 128:(tt + 1) * 128], identity=ident[:]),
                                        reads=["xg", "ident"], writes=[ppn])
                                if q % 2 == 0:
                                    mk.op("dve", lambda e, yi=yi, pp=pp: e.tensor_copy(out=yo[yi][:].rearrange("p a b -> p (a b)"), in_=pp[:, :]),
                                          reads=[ppn], writes=[f"yo{yi}"])
                                else:
                                    mk.op("act", lambda e, yi=yi, pp=pp: e.copy(out=yo[yi][:].rearrange("p a b -> p (a b)"), in_=pp[:, :]),
                                          reads=[ppn], writes=[f"yo{yi}"])
                                mk.dma("pool", f"yo{yi}", out_d[t0 + tt * 128:t0 + (tt + 1) * 128, q * 512:(q + 1) * 512],
                                       yo[yi][:].rearrange("p a b -> p (a b)"), reads=[f"yo{yi}"])
                mk.sync()

        post_phase(0, ws_out0, False)
        if stop_after == 6:
            mk.emit()
            return nc, dbg

        with ExitStack() as ph:
            def sbp(name, shape, dt):
                uid[0] += 1
                return ph.enter_context(nc.sbuf_tensor(f"sb{uid[0]}_{name}", list(shape), dt))
            xg = sbp("xg", [128, 16, 512], F32)
            sq = sbp("sq", [128, 16, 512], BF16)
            hT = sbp("hT", [128, 16, 512], BF16)
            rs = sbp("rs", [128, 512], F32)
            tmpa = [(sbp(f"tmpa{i}", [128, 512], F32), f"tmpa{i}") for i in range(2)]
            tm = [[(sbp(f"rt{a}{i}", [128, 512], F32), f"rt{a}{i}") for i in range(2)] for a in range(3)]
            wbufs = [sbp(f"wbuf{i}", [128, 8192], BF16) for i in range(3)]
            stb = [sbp(f"stb{i}", [128, 512], BF16) for i in range(4)]
            cosd = sbp("cosd", [128, NT], F32)
            sind = sbp("sind", [128, NT], F32)
            perm128 = sbp("perm128", [128, 128], F32)
            mk.dma("sp", "k0", cosd[:], cosd_in, writes=["ropetab"])
            mk.dma("sp", "k1", sind[:], sind_in, writes=["ropetab"])
            mk.dma("sp", "k2", perm128[:], perm128_in, writes=["perm"])
            cnt = [0]
            for gi, (t0, T) in enumerate(GROUPS):
                m = 0 if t0 < NLAT else 1
                mk.dma("sp", "xg", xg[:, :, 0:T], XT[:, :, t0:t0 + T].rearrange("j p t -> p j t"), writes=["xg"])
                norm_fm(xg, "xg", 16, T, D, sq, "sq", rs, "rs",
                        lambda j, m=m: AS[:, 1, 0, m, j:j + 1], lambda j, m=m: modv(1, 0, m, j),
                        lambda j, T=T: hT[:, j, 0:T], "hT", tmp=tmpa)
                for bi, (c0, ncol) in enumerate(ws_in1.blocks):
                    if c0 < 2048 and m == 1:
                        continue
                    wv, wn = load_w(wbufs, ws_in1, bi)
                    if c0 >= 4096:
                        def epi_v(t, ps, psn, c0=c0, t0=t0):
                            i = cnt[0] % 4; cnt[0] += 1
                            mk.op("act", lambda e: e.copy(out=stb[i][:, :], in_=ps[:, :]), reads=[psn], writes=[f"stb{i}"])
                            store(VD[t0 + t * 128:t0 + (t + 1) * 128, c0 - 4096:c0 - 4096 + 512], stb[i][:, :], f"stb{i}", ("VD", t0))
                        proj_tm(wv, wn, hT, "hT", 16, ncol, T, epi_v)
                        continue

                    def epi(c, ps, psn, M, c0=c0, t0=t0, T=T):
                        col = c0 + c * 128
                        i = cnt[0] % 4; cnt[0] += 1
                        rope_fm(ps, psn, 128, T, t0, cosd, sind, perm128, stb[i][:, 0:T], f"stb{i}", tm)
                        if col < 2048:
                            store(QD[col // 128, :, t0:t0 + T], stb[i][:, 0:T], f"stb{i}", ("QD", t0))
                        else:
                            store(KD[(col - 2048) // 128, :, t0:t0 + T], stb[i][:, 0:T], f"stb{i}", ("KD", t0))
                    proj_fm(wv, wn, hT, "hT", 16, ncol, T, epi, banks=(0, 1, 2, 3))
            mk.sync()
        if stop_after == 7:
            mk.emit()
            return nc, dbg

        with ExitStack() as ph:
            def sbp(name, shape, dt):
                uid[0] += 1
                return ph.enter_context(nc.sbuf_tensor(f"sb{uid[0]}_{name}", list(shape), dt))
            sets = []
            for si in range(2):
                sets.append(dict(q1=sbp(f"q1{si}", [128, NLAT], BF16), q2=sbp(f"q2{si}", [128, NLAT], BF16),
                                 k1=sbp(f"k1{si}", [128, NT], BF16), k2=sbp(f"k2{si}", [128, NT], BF16),
                                 v=sbp(f"vaugd{si}", [128, 34, 257], BF16)))
            pTb = [(sbp(f"pT{i}", [128, 512], BF16), f"pT{i}") for i in range(3)]
            rec = sbp("rec", [128, 8], F32)
            o1 = sbp("o1", [128, 4, 256], F32)
            oo = sbp("oo", [128, 4, 256], F32)
            junk = sbp("junk", [128, 256], F32)
            ssq = sbp("ssq", [128, 8], F32)
            ob = sbp("obd", [128, 4, 256], BF16)
            oT = [sbp(f"oT{i}", [128, 512], BF16) for i in range(2)]
            for si in range(2):
                mk.op("dve", lambda e, si=si: e.memset(sets[si]["v"][:, :, 256:257], 1.0), writes=[f"vaug{si}"])

            def load_head(h):
                si = h % 2
                S_ = sets[si]
                key = f"qa{si}"
                mk.dma("sp", key + "a", S_["q1"][:, :], QD[2 * h], writes=[f"q1{si}"])
                mk.dma("sp", key + "b", S_["k1"][:, :], KD[2 * h], writes=[f"k1{si}"])
                mk.dma("sp", key + "c", S_["v"][:, :, 0:256], VD[:, h * 256:(h + 1) * 256].rearrange("(t p) c -> p t c", p=128), writes=[f"vaug{si}"])
                mk.dma("sp", key + "d", S_["q2"][:, :], QD[2 * h + 1], writes=[f"q2{si}"])
                mk.dma("sp", key + "e", S_["k2"][:, :], KD[2 * h + 1], writes=[f"k2{si}"])

            load_head(0)
            for h in range(8):
                if h + 1 < 8:
                    load_head(h + 1)
                si = h % 2
                q1 = sets[si]["q1"]; q2 = sets[si]["q2"]; k1 = sets[si]["k1"]; k2 = sets[si]["k2"]; vaug = sets[si]["v"]
                q1n, q2n, k1n, k2n, vgn = f"q1{si}", f"q2{si}", f"k1{si}", f"k2{si}", f"vaug{si}"
                for g in range(8):
                    t0 = g * 512
                    kt = list(range(34))
                    accs = attn_chunk([q1[:, t0:t0 + 512]], [lambda t, k1=k1: k1[:, t * 128:(t + 1) * 128]], [q1n, k1n],
                                      lambda t, vaug=vaug: vaug[:, t, :], vgn, 256, kt, 512, 128 ** -0.5, pTb)
                    for u, (acc, accn) in enumerate(accs):
                        mk.op("dve", lambda e, u=u, acc=acc: e.reciprocal(out=rec[:, u:u + 1], in_=acc[:, 256:257]), reads=[accn], writes=[("rec", u)])
                        mk.op("dve", lambda e, u=u, acc=acc: e.tensor_scalar(out=o1[:, u, :], in0=acc[:, 0:256], scalar1=rec[:, u:u + 1], scalar2=None,
                                                                             op0=ALU.mult), reads=[accn, ("rec", u)], writes=[("o1", u)])
                    accs = attn_chunk([q2[:, t0:t0 + 512]], [lambda t, k2=k2: k2[:, t * 128:(t + 1) * 128]], [q2n, k2n],
                                      lambda t, vaug=vaug: vaug[:, t, :], vgn, 256, kt, 512, 128 ** -0.5, pTb)
                    for u, (acc, accn) in enumerate(accs):
                        mk.op("dve", lambda e, u=u, acc=acc: e.reciprocal(out=rec[:, 4 + u:5 + u], in_=acc[:, 256:257]), reads=[accn], writes=[("rec2", u)])
                        mk.op("dve", lambda e, u=u: e.tensor_scalar(out=rec[:, 4 + u:5 + u], in0=rec[:, 4 + u:5 + u], scalar1=lamt[:, 1:2], scalar2=None,
                                                                    op0=ALU.mult), reads=[("rec2", u)], writes=[("rec2", u)])
                        mk.op("dve", lambda e, u=u, acc=acc: e.scalar_tensor_tensor(out=oo[:, u, :], in0=acc[:, 0:256], scalar=rec[:, 4 + u:5 + u],
                                                                                     in1=o1[:, u, :], op0=ALU.mult, op1=ALU.add),
                              reads=[accn, ("rec2", u), ("o1", u)], writes=[("oo", u)])
                        mk.op("act", lambda e, u=u: e.activation(out=junk[:, :], in_=oo[:, u, :], func=AF.Square, accum_out=ssq[:, u:u + 1]),
                              reads=[("oo", u)], writes=["junk", ("ssq", u)])
                        mk.op("act", lambda e, u=u: e.activation(out=ssq[:, 4 + u:5 + u], in_=ssq[:, u:u + 1], func=AF.Sqrt, bias=epst[:, 0:1],
                                                                 scale=1.0 / 256), reads=[("ssq", u)], writes=[("ssr", u)])
                        mk.op("dve", lambda e, u=u: e.reciprocal(out=ssq[:, 4 + u:5 + u], in_=ssq[:, 4 + u:5 + u]), reads=[("ssr", u)], writes=[("ssr", u)])
                        mk.op("dve", lambda e, u=u: e.tensor_scalar(out=ob[:, u, :], in0=oo[:, u, :], scalar1=ssq[:, 4 + u:5 + u], scalar2=1.0 - lam_init,
                                                                    op0=ALU.mult, op1=ALU.mult), reads=[("oo", u), ("ssr", u)], writes=[("obd", u)])
                        for c in range(2):
                            mk.op("pe", lambda e, u=u, c=c: e.transpose(out=PTt[:, c * 512 + u * 128:c * 512 + (u + 1) * 128],
                                                                        in_=ob[:, u, c * 128:(c + 1) * 128], identity=identb[:]),
                                  reads=[("obd", u), "identb"], writes=["ptb"])
                    for c in range(2):
                        mk.op("act", lambda e, c=c: e.activation(out=oT[c][:, :], in_=PTt[:, c * 512:(c + 1) * 512], func=AF.Copy,
                                                                 scale=subgT[:, c:c + 1]), reads=["ptb"], writes=[f"oT{c}"])
                        store(OT[2 * h + c, :, t0:t0 + 512], oT[c][:, :], f"oT{c}", ("OT", 2 * h + c))
            mk.sync()
        if stop_after == 8:
            mk.emit()
            return nc, dbg
        post_phase(1, ws_out1, True)
        mk.emit()
    return nc, dbg


def _host_inputs(inputs, b):
    f = np.float32
    x = np.ascontiguousarray(inputs["x"][b], dtype=f)
    ctx = np.ascontiguousarray(inputs["ctx"][b], dtype=f)
    cvec = np.stack([inputs["c"][b], inputs["c_ctx"]]).astype(f)
    gt = np.concatenate([inputs["norm1_g"][0], inputs["norm1_g"][1], inputs["norm2_g"][0], inputs["norm2_g"][1],
                         inputs["final_g"]]).astype(f).reshape(80, 128)
    g5 = np.concatenate([inputs["ab_g_cq"][0], inputs["ab_g_ckv"][0]]).astype(f).reshape(8, 128)
    subg = inputs["diff_subln_g"][0].astype(f).reshape(2, 128)
    lvec = np.stack([inputs["diff_lq1"][0], inputs["diff_lk1"][0], inputs["diff_lq2"][0], inputs["diff_lk2"][0]], axis=1).astype(f)
    rpb = inputs["ab_rpb"][0].astype(f)
    bt = np.empty((8, NBT, 128, 128), f)
    for ti, (valid, dr, dc) in enumerate(NA_TILES):
        g = rpb[:, dr, dc]
        bt[:, ti] = np.where(valid[None], g, f(NEG))
    cosm, sinm, perm64 = _rope_tables(64)
    cosd, sind, perm128 = _rope_tables(128)
    return {
        "x": x, "ctx": ctx, "cvec": cvec, "ada_w": np.ascontiguousarray(inputs["ada_w"], dtype=f),
        "ada_b": np.ascontiguousarray(inputs["ada_b"], dtype=f), "gtab": gt, "g512": g5, "subg": subg, "lvec": lvec,
        "btab": bt, "cosm": cosm, "sinm": sinm, "perm64": perm64, "cosd": cosd, "sind": sind, "perm128": perm128,
        "ident": np.eye(128, dtype=f),
        "ab_w_in": np.ascontiguousarray(inputs["ab_w_in"][0], dtype=f), "ab_w_uq": np.ascontiguousarray(inputs["ab_w_uq"][0], dtype=f),
        "ab_w_ukv": np.ascontiguousarray(inputs["ab_w_ukv"][0], dtype=f), "ab_w_out": np.ascontiguousarray(inputs["ab_w_out"][0], dtype=f),
        "mlp_w1": np.ascontiguousarray(inputs["mlp_w1"], dtype=f), "mlp_w2": np.ascontiguousarray(inputs["mlp_w2"], dtype=f),
        "diff_w_in": np.ascontiguousarray(inputs["diff_w_in"][0], dtype=f), "diff_w_out": np.ascontiguousarray(inputs["diff_w_out"][0], dtype=f),
    }


def kernel(**inputs):
    inputs = {k: np.asarray(v) for k, v in inputs.items()}
    nc, _ = build()
    in_maps = [_host_inputs(inputs, b) for b in range(N_CORES)]
    res = run_bass_kernel_spmd(nc, in_maps, core_ids=list(range(N_CORES)))
    out = np.stack([np.asarray(res.results[b]["out"], dtype=np.float32) for b in range(N_CORES)], axis=0)
    return out
```
